# Optimizing a Trainium2 kernel written in Bass

```python
import jax, jax.numpy as jnp
from jax import lax
import numpy as np

D_MODEL = 1024
BATCH = 8
SEQ = 2048
DEPTH = 1
DEC_BATCH = 128
DEC_SEQ = 8
PAST_LEN = 16384
PAGE_SIZE = 128

POOL_WINDOWS = (2, 4, 8, 16)
N_POOL_GROUPS = len(POOL_WINDOWS)
POOL_WIDTH = D_MODEL // 2
POOL_GROUP = POOL_WIDTH // N_POOL_GROUPS
POOL_HIST = max(POOL_WINDOWS) - 1
CONV_WIDTH = D_MODEL // 2
CONV_K = 3
CONV_HIST = CONV_K - 1
D_FF = -(-8 * D_MODEL // (3 * 256)) * 256
D_PLE = 256
EPS = 1e-6
IN_WIDTH = POOL_WIDTH + 3 * CONV_WIDTH + 2 * D_MODEL

kernel_name = "pool_shortconv_gated_hybrid_step"


def rmsnorm(x, g):
    xf = x.astype(jnp.float32)
    y = xf * lax.rsqrt(jnp.mean(xf * xf, axis=-1, keepdims=True) + EPS)
    return (y * g.astype(jnp.float32)).astype(x.dtype)


def pool_branch(u, prefix, n_valid, w_group, scale):
    B, T, _ = u.shape
    ext = jnp.concatenate([prefix.astype(u.dtype), u], axis=1)
    cs = jnp.cumsum(ext.astype(jnp.float32), axis=1)
    cs0 = jnp.concatenate([jnp.zeros((B, 1, POOL_WIDTH), jnp.float32), cs], axis=1)
    t = jnp.arange(T)
    H = POOL_HIST
    means = []
    for gi, w in enumerate(POOL_WINDOWS):
        c0, c1 = gi * POOL_GROUP, (gi + 1) * POOL_GROUP
        win = cs0[:, H + 1:H + 1 + T, c0:c1] - cs0[:, H + 1 - w:H + 1 - w + T, c0:c1]
        cnt = jnp.minimum(w, t + 1 + n_valid).astype(jnp.float32)[None, :, None]
        means.append(win / cnt)
    d = (jnp.concatenate(means, axis=-1) - u.astype(jnp.float32)).astype(u.dtype)
    d = d.reshape(B, T, N_POOL_GROUPS, POOL_GROUP)
    mixed = jnp.einsum('btgc,gcd->btgd', d, w_group).reshape(B, T, POOL_WIDTH) * scale
    return mixed, ext[:, -POOL_HIST:]


def conv_branch(b, c, h, prefix, w_conv):
    T = b.shape[1]
    v = c * h
    ext = jnp.concatenate([prefix.astype(v.dtype), v], axis=1)
    y = sum(w_conv[k] * ext[:, k:k + T] for k in range(CONV_K))
    return b * y, ext[:, -CONV_HIST:]


def layer(x, p_l, pool_prefix, pool_valid, conv_prefix,
          g_mix, w_in, w_pool_group, pool_scale, w_pool_up, w_conv, w_conv_out, w_o,
          g_ffn, w_ffn_in, w_ffn_out, g_ple, w_ple, w_ple_gate):
    hn = rmsnorm(x, g_mix)
    z = hn @ w_in
    o = 0
    u = z[..., o:o + POOL_WIDTH]; o += POOL_WIDTH
    bg = z[..., o:o + CONV_WIDTH]; o += CONV_WIDTH
    cg = z[..., o:o + CONV_WIDTH]; o += CONV_WIDTH
    hv = z[..., o:o + CONV_WIDTH]; o += CONV_WIDTH
    gp = z[..., o:o + D_MODEL]; o += D_MODEL
    gc = z[..., o:o + D_MODEL]
    pool_out, pool_state = pool_branch(u, pool_prefix, pool_valid, w_pool_group, pool_scale)
    conv_out, conv_state = conv_branch(bg, cg, hv, conv_prefix, w_conv)
    merged = jax.nn.sigmoid(gp) * (pool_out @ w_pool_up) + jax.nn.sigmoid(gc) * (conv_out @ w_conv_out)
    x = x + merged @ w_o
    hn = rmsnorm(x, g_ffn)
    gu = hn @ w_ffn_in
    x = x + (jax.nn.silu(gu[..., :D_FF]) * gu[..., D_FF:]) @ w_ffn_out
    x = x + jax.nn.sigmoid(rmsnorm(x, g_ple) @ w_ple_gate) * (p_l @ w_ple)
    return x, pool_state, conv_state


def setup_inputs(seed: int = 0) -> dict:
    key = jax.random.key(seed)
    ks = jax.random.split(key, 24)
    nrm = lambda k, s, sc: jax.random.normal(k, s, jnp.float32) * sc
    gain = lambda k, s: 1.0 + 0.05 * jax.random.normal(k, s, jnp.float32)
    return {
        "x_prompt": nrm(ks[0], (BATCH, SEQ, D_MODEL), 1.0),
        "x_sample": nrm(ks[1], (DEC_BATCH, DEC_SEQ, D_MODEL), 1.0),
        "state_pool": nrm(ks[2], (DEPTH, DEC_BATCH, POOL_HIST, POOL_WIDTH), 1.0),
        "state_conv": nrm(ks[3], (DEPTH, DEC_BATCH, CONV_HIST, CONV_WIDTH), 1.0),
        "p_prompt": nrm(ks[4], (DEPTH, BATCH, SEQ, D_PLE), 1.0),
        "p_sample": nrm(ks[5], (DEPTH, DEC_BATCH, DEC_SEQ, D_PLE), 1.0),
        "g_mix": gain(ks[6], (DEPTH, D_MODEL)),
        "w_in": nrm(ks[7], (DEPTH, D_MODEL, IN_WIDTH), D_MODEL ** -0.5),
        "w_pool_group": nrm(ks[8], (DEPTH, N_POOL_GROUPS, POOL_GROUP, POOL_GROUP), POOL_GROUP ** -0.5),
        "pool_scale": gain(ks[9], (DEPTH, POOL_WIDTH)),
        "w_pool_up": nrm(ks[10], (DEPTH, POOL_WIDTH, D_MODEL), POOL_WIDTH ** -0.5),
        "w_conv": nrm(ks[11], (DEPTH, CONV_K, CONV_WIDTH), CONV_K ** -0.5),
        "w_conv_out": nrm(ks[12], (DEPTH, CONV_WIDTH, D_MODEL), CONV_WIDTH ** -0.5),
        "w_o": nrm(ks[13], (DEPTH, D_MODEL, D_MODEL), D_MODEL ** -0.5),
        "g_ffn": gain(ks[14], (DEPTH, D_MODEL)),
        "w_ffn_in": nrm(ks[15], (DEPTH, D_MODEL, 2 * D_FF), D_MODEL ** -0.5),
        "w_ffn_out": nrm(ks[16], (DEPTH, D_FF, D_MODEL), D_FF ** -0.5),
        "g_ple": gain(ks[17], (DEPTH, D_MODEL)),
        "w_ple": nrm(ks[18], (DEPTH, D_PLE, D_MODEL), D_PLE ** -0.5),
        "w_ple_gate": nrm(ks[19], (DEPTH, D_MODEL, D_MODEL), D_MODEL ** -0.5),
        "g_final": gain(ks[20], (D_MODEL,)),
    }


def reference(x_prompt, x_sample, state_pool, state_conv, p_prompt, p_sample,
              g_mix, w_in, w_pool_group, pool_scale, w_pool_up, w_conv, w_conv_out, w_o,
              g_ffn, w_ffn_in, w_ffn_out, g_ple, w_ple, w_ple_gate, g_final):
    xp, xs = x_prompt, x_sample
    sample_pool_valid = min(PAST_LEN, POOL_HIST)
    pool_p, conv_p, pool_s, conv_s = [], [], [], []
    for i in range(DEPTH):
        wts = (g_mix[i], w_in[i], w_pool_group[i], pool_scale[i], w_pool_up[i], w_conv[i],
               w_conv_out[i], w_o[i], g_ffn[i], w_ffn_in[i], w_ffn_out[i], g_ple[i],
               w_ple[i], w_ple_gate[i])
        zp_pool = jnp.zeros((xp.shape[0], POOL_HIST, POOL_WIDTH), xp.dtype)
        zp_conv = jnp.zeros((xp.shape[0], CONV_HIST, CONV_WIDTH), xp.dtype)
        xp, sp, sc = layer(xp, p_prompt[i], zp_pool, 0, zp_conv, *wts)
        pool_p.append(sp); conv_p.append(sc)
        xs, sp, sc = layer(xs, p_sample[i], state_pool[i], sample_pool_valid, state_conv[i], *wts)
        pool_s.append(sp); conv_s.append(sc)
    y_prompt = rmsnorm(xp, g_final)
    y_sample = rmsnorm(xs, g_final)
    new_pool_prompt = jnp.stack(pool_p, axis=0)
    new_conv_prompt = jnp.stack(conv_p, axis=0)
    new_pool_sample = jnp.stack(pool_s, axis=0)
    new_conv_sample = jnp.stack(conv_s, axis=0)
    return (y_prompt, y_sample, new_pool_prompt, new_conv_prompt, new_pool_sample, new_conv_sample)
```

```python
import numpy as np
import concourse.bass as bass
import concourse.mybir as mybir
from concourse.bass_utils import run_bass_kernel_spmd

F32 = mybir.dt.float32
BF16 = mybir.dt.bfloat16
ALU = mybir.AluOpType
AF = mybir.ActivationFunctionType

D = 1024
SEQ = 2048
NT = 17
NTOK = NT * 128
NB = [512, 512, 512, 512, 128]
CB = [0, 512, 1024, 1536, 2048]
DFF = 2816
NF = 22
EPS = 1e-6
SB_BASE = 16640
SB_END = 229376


class Sem:
    def __init__(self, nc, name):
        self.h = nc.alloc_semaphore(name)
        self.cnt = 0


class Res:
    __slots__ = ("name", "w", "r")

    def __init__(self, name="r"):
        self.name = name
        self.w = []
        self.r = []

    def inherit(self, *others):
        for o in others:
            self.w = self.w + o.w + o.r
        return self


class Em:
    def __init__(self, nc):
        self.nc = nc
        self.engs = {}
        for n in ["pe", "act", "dve", "pool", "sp"]:
            self.engs[n] = dict(sem=Sem(nc, "s_" + n), seen={}, thunks=[])
        self.final = []
        self.nsem = 5

    def newsem(self, name):
        self.nsem += 1
        return Sem(self.nc, name)

    def _waits(self, e, reads, writes):
        E = self.engs[e]
        need = {}
        for r in reads:
            for (s, v) in r.w:
                if need.get(s, 0) < v:
                    need[s] = v
        for w in writes:
            for (s, v) in w.w:
                if need.get(s, 0) < v:
                    need[s] = v
            for (s, v) in w.r:
                if need.get(s, 0) < v:
                    need[s] = v
        out = []
        for s, v in need.items():
            if s is E["sem"]:
                if e == "pe":
                    continue
                if v > s.cnt:
                    continue
            if E["seen"].get(s, 0) < v:
                E["seen"][s] = v
                out.append((s, v))
        return out

    @staticmethod
    def _mark(tok, reads, writes):
        for r in reads:
            r.r = [t for t in r.r if t[0] is not tok[0]] + [tok]
        for w in writes:
            w.w = [tok]
            w.r = []

    def op(self, e, meth, kw, reads=(), writes=(), sig=True):
        E = self.engs[e]
        fn = (lambda eng, meth=meth, kw=kw: getattr(eng, meth)(**kw))
        waits = self._waits(e, reads, writes)
        s = E["sem"]
        if sig:
            s.cnt += 1
            tok = (s, s.cnt)
        else:
            tok = (s, s.cnt + 1)

        def thunk(eng, waits=waits, fn=fn, sig=sig, s=s):
            for (ws, wv) in waits:
                eng.wait_ge(ws.h, wv)
            ins = fn(eng)
            if sig:
                ins.then_inc(s.h, 1)
        E["thunks"].append(thunk)
        self._mark(tok, reads, writes)
        return tok

    def dma(self, q, out, in_, dsem, reads=(), writes=(), final=False, **kw):
        E = self.engs[q]
        waits = self._waits(q, reads, writes)
        dsem.cnt += 16
        tok = (dsem, dsem.cnt)

        def thunk(eng, waits=waits, out=out, in_=in_, kw=kw, dsem=dsem):
            for (ws, wv) in waits:
                eng.wait_ge(ws.h, wv)
            eng.dma_start(out=out, in_=in_, **kw).then_inc(dsem.h, 16)
        E["thunks"].append(thunk)
        self._mark(tok, reads, writes)
        if final:
            self.final.append(tok)
        return tok

    def finish(self):
        need = {}
        for (s, v) in self.final:
            need[s] = max(need.get(s, 0), v)
        lst = list(need.items())

        def thunk(eng):
            for (s, v) in lst:
                eng.wait_ge(s.h, v)
        self.engs["sp"]["thunks"].append(thunk)

    def run(self):
        nc = self.nc
        for n, E in self.engs.items():
            assert E["sem"].cnt < 60000, (n, E["sem"].cnt)
        with nc.Block() as block:
            @block.tensor
            def _(eng):
                for t in self.engs["pe"]["thunks"]:
                    t(eng)

            @block.scalar
            def _(eng):
                for t in self.engs["act"]["thunks"]:
                    t(eng)

            @block.vector
            def _(eng):
                for t in self.engs["dve"]["thunks"]:
                    t(eng)

            @block.gpsimd
            def _(eng):
                for t in self.engs["pool"]["thunks"]:
                    t(eng)

            @block.sync
            def _(eng):
                for t in self.engs["sp"]["thunks"]:
                    t(eng)


def build_nc():
    nc = bass.Bass("TRN2", target_bir_lowering=False)

    def din(name, shape):
        return nc.dram_tensor(name, list(shape), F32, kind="ExternalInput").ap()

    def dout(name, shape):
        return nc.dram_tensor(name, list(shape), F32, kind="ExternalOutput").ap()

    xp = din("xp", [SEQ, D]); xs = din("xs", [128, D])
    spool = din("spool", [16, 15, 512]); sconv = din("sconv", [16, 2, 512])
    ppr = din("ppr", [SEQ, 256]); psa = din("psa", [128, 256])
    g_mix = din("g_mix", [D]); w_in = din("w_in", [D, 4096])
    w_pg = din("w_pg", [4, 128, 128]); pool_scale = din("pool_scale", [512])
    w_pool_up = din("w_pool_up", [512, D]); w_conv = din("w_conv", [3, 512])
    w_conv_out = din("w_conv_out", [512, D]); w_o = din("w_o", [D, D])
    g_ffn = din("g_ffn", [D]); w_ffn_in = din("w_ffn_in", [D, 2 * DFF])
    w_ffn_out = din("w_ffn_out", [DFF, D]); g_ple = din("g_ple", [D])
    w_ple = din("w_ple", [256, D]); w_ple_gate = din("w_ple_gate", [D, D])
    g_final = din("g_final", [D])

    yp = dout("yp", [SEQ, D]); ys = dout("ys", [128, D])
    npp = dout("npp", [15, 512]); ncp = dout("ncp", [2, 512])
    nps = dout("nps", [16, 15, 512]); ncs = dout("ncs", [16, 2, 512])

    em = Em(nc)

    def xtile(t):
        return xp[t * 128:(t + 1) * 128, :] if t < 16 else xs[:, :]

    def ytile(t):
        return yp[t * 128:(t + 1) * 128, :] if t < 16 else ys[:, :]

    def ptile(t):
        return ppr[t * 128:(t + 1) * 128, :] if t < 16 else psa[:, :]

    cnt = [0]

    def sbt(off, shape, dt):
        nbytes = int(np.prod(shape[1:])) * (4 if dt == F32 else 2)
        assert off % 32 == 0, off
        assert SB_BASE <= off and off + nbytes <= SB_END, (off, nbytes)
        cnt[0] += 1
        return nc.alloc_sbuf_tensor_at("t%d" % cnt[0], list(shape), dt, offset=off), off + nbytes

    A0 = SB_BASE
    B0 = A0 + NT * 4096
    C0 = B0 + 8 * NTOK * 2
    D0 = C0 + 69632
    assert D0 + 38656 <= SB_END

    x_res, _ = sbt(A0, [128, NT, D], F32)
    hnT, _ = sbt(B0, [128, 8, NTOK], BF16)
    pool_out, _ = sbt(C0, [128, 4, NTOK], BF16)
    conv_out, _ = sbt(C0 + 17408, [128, 4, NTOK], BF16)
    merged, _ = sbt(C0 + 34816, [128, 8, NTOK], BF16)
    NSLOT = 8
    wslot = []
    o = D0
    for i in range(NSLOT):
        t_, o = sbt(o, [128, 8, 128], BF16)
        wslot.append(t_)
    wo_sb, o = sbt(o, [128, 8, D], BF16)
    WO_OFF = o - 16384
    ident, o = sbt(o, [128, 128], BF16)
    identf, o = sbt(o, [128, 128], F32)
    gTall, o = sbt(o, [128, 32], F32)
    gT = [gTall[:, 8 * i:8 * i + 8] for i in range(3)]
    pscale = gTall[:, 24:32]
    cst1, o = sbt(o, [128, 128], F32)
    cst2, o = sbt(o, [128, 128], F32)
    wconv, o = sbt(o, [128, 4, 4], F32)
    wg_sb, o = sbt(o, [128, 4, 128], BF16)
    rcnt, o = sbt(o, [128, 4, 16], F32)
    stat, o = sbt(o, [128, 4 * NT * 2], F32)
    hdbuf, o = sbt(o, [128, 16], F32)
    assert o <= SB_END, o

    LU = 15 + SEQ + 16 * 23
    UB = 9728
    o = A0
    ubuf = []
    for i in range(2):
        t_, o2 = sbt(o, [128, LU], F32)
        ubuf.append(t_); o += UB
    tbuf = []
    for i in range(2):
        t_, o2 = sbt(o, [128, LU], F32)
        tbuf.append(t_); o += UB
    dbuf = []
    for i in range(2):
        t_, o = sbt(o, [128, NTOK], BF16)
        dbuf.append(t_)
    LV = 2 + SEQ + 16 * 10
    vb_, o = sbt(o, [128, LV], F32)
    o = (o + 31) // 32 * 32
    vbuf = [vb_, vb_]
    csb = []
    for i in range(2):
        t_, o = sbt(o, [128, 512], F32)
        csb.append(t_)
    ybuf = []
    for i in range(2):
        t_, o = sbt(o, [128, 512], F32)
        ybuf.append(t_)
    ybuf = ybuf + ybuf
    ctm, o = sbt(o, [128, 2, 128], F32)
    assert o <= B0, o
    o = A0 + 2 * UB
    s4buf = []
    for i in range(8):
        t_, o = sbt(o, [128, 512], F32)
        s4buf.append(t_)
    assert o <= A0 + 4 * UB
    o = C0 + 34816
    NX1 = 8
    xin1 = []
    for i in range(4):
        t_, o = sbt(o, [128, D], F32)
        xin1.append(t_)
    for i in range(4):
        t_, _ = sbt(C0 + i * 4096, [128, D], F32)
        xin1.append(t_)
    xn1 = []
    for i in range(2):
        t_, o = sbt(o, [128, D], BF16)
        xn1.append(t_)
    stp = []
    for i in range(2):
        t_, o = sbt(o, [128, 512], F32)
        stp.append(t_)
    stc, o = sbt(o, [128, 512], F32)
    ustage, o = sbt(o, [128, 2, 512], F32)
    vstage, o = sbt(o, [128, 2, 512], F32)
    assert o <= C0 + 69632, o
    o = C0 + 17408
    xin5 = []
    for i in range(2):
        t_, o = sbt(o, [128, D], F32)
        xin5.append(t_)
    xn5 = []
    for i in range(3):
        t_, o = sbt(o, [128, D], BF16)
        xn5.append(t_)
    assert o <= C0 + 34816
    NPART = [3, 3, 4, 4, 4, 4]
    act_sb, _ = sbt(C0, [128, 4, NTOK], BF16)
    o = C0 + 17408
    sg6 = []
    for i in range(2):
        t_, o = sbt(o, [128, 512], F32)
        sg6.append(t_)
    xn6 = []
    for i in range(3):
        t_, o = sbt(o, [128, D], BF16)
        xn6.append(t_)
    assert o <= C0 + 34816
    o = C0 + 34816
    wfo = []
    for i in range(2):
        t_, o = sbt(o, [128, 4, D], BF16)
        wfo.append(t_)
    wpg_sb, o = sbt(o, [128, 8, D], BF16)
    assert o <= D0
    o = C0
    sg7 = []
    for i in range(2):
        t_, o = sbt(o, [128, D], F32)
        sg7.append(t_)
    tt7 = []
    for i in range(2):
        t_, o = sbt(o, [128, D], F32)
        tt7.append(t_)
    yst = []
    for i in range(2):
        t_, o = sbt(o, [128, D], F32)
        yst.append(t_)
    assert o <= C0 + 34816
    wple_sb, o = sbt(WO_OFF, [128, 2, D], BF16)
    gfin, o = sbt(o, [128, D], F32)
    pin = []
    for i in range(4):
        t_, o = sbt(o, [128, 256], F32)
        pin.append(t_)
    pbf = []
    for i in range(3):
        t_, o = sbt(o, [128, 256], BF16)
        pbf.append(t_)
    pT = []
    for i in range(2):
        t_, o = sbt(o, [128, 2, 128], BF16)
        pT.append(t_)
    assert o <= WO_OFF + 16384

    psum = nc.alloc_psum_tensor("psum", [128, 8, 512], F32)
    psum_bf = psum.bitcast(BF16)

    Rps = [Res("ps%d" % i) for i in range(8)]
    Rh = [Res() for _ in range(NT)]
    Rx = [Res() for _ in range(NT)]
    Rpo = [[Res() for _ in range(5)] for _ in range(4)]
    Rco = [[Res() for _ in range(5)] for _ in range(4)]
    Rm = [[Res() for _ in range(5)] for _ in range(8)]
    Rw = [Res() for _ in range(NSLOT)]
    Dw = [em.newsem("dw%d" % i) for i in range(NSLOT)]
    Rwo = Res(); Dwo = em.newsem("dwo")
    Rconst = Res(); Dconst = em.newsem("dconst")
    Rident = Res(); Ridentf = Res(); Rrcnt = Res()
    Ru = [Res(), Res()]; Rt = [Res(), Res()]; Rd = [Res(), Res()]
    Rust = Res(); Rvst = Res(); Dust = em.newsem("dust"); Dvst = em.newsem("dvst")
    Dout = em.newsem("dout")
    Dmisc = em.newsem("dmisc")

    bankc = [0]

    def nb():
        b = bankc[0] % 8
        bankc[0] += 1
        return b

    def nb2():
        if bankc[0] % 2:
            bankc[0] += 1
        b = bankc[0] % 8
        bankc[0] += 2
        return b

    wcol = lambda src, c0: src[:, c0:c0 + 128].rearrange("(k p) n -> p k n", p=128)
    WL = []
    def wl_s3(cc):
        WL.append((wcol(w_in, 512 + cc * 128), 8))
        WL.append((wcol(w_in, 1024 + cc * 128), 8))
        WL.append((wcol(w_in, 1536 + cc * 128), 8))
    wl_u = lambda g: WL.append((wcol(w_in, g * 128), 8))
    wl_u(0); wl_s3(0); wl_u(1); wl_s3(1); wl_u(2); wl_u(3); wl_s3(2); wl_s3(3)
    for m_ in range(8):
        WL.append((wcol(w_in, 2048 + m_ * 128), 8))
        WL.append((wcol(w_in, 3072 + m_ * 128), 8))
        WL.append((wcol(w_pool_up, m_ * 128), 4))
        WL.append((wcol(w_conv_out, m_ * 128), 4))
    for f_ in range(NF):
        WL.append((wcol(w_ffn_in, f_ * 128), 8))
        WL.append((wcol(w_ffn_in, DFF + f_ * 128), 8))
    wst = dict(issued=0, taken=0, done=0)

    def w_issue(limit=None):
        while wst["issued"] < len(WL) and wst["issued"] < wst["done"] + NSLOT and (limit is None or wst["issued"] < limit):
            j = wst["issued"]
            src_ap, kn = WL[j]
            em.dma("pool", wslot[j % NSLOT][:, 0:kn, :], src_ap, Dw[j % NSLOT], writes=[Rw[j % NSLOT]])
            wst["issued"] += 1

    slot_idx = {}
    wflags = [False] * len(WL)

    def load_w():
        j = wst["taken"]
        wst["taken"] += 1
        assert j < wst["issued"], (j, wst)
        slot_idx[j % NSLOT] = j
        return j % NSLOT

    def w_done(*slots):
        for sl in slots:
            wflags[slot_idx[sl]] = True
        while wst["done"] < len(WL) and wflags[wst["done"]]:
            wst["done"] += 1

    def tiles_of(tb):
        return list(range(4 * tb, 4 * tb + 4)) if tb < 4 else [16]

    def tb_of(t):
        return t // 4 if t < 16 else 4

    def cdma(out, in_, **kw):
        em.dma("act", out, in_, Dconst, writes=[Rconst], **kw)
        Rconst.w = []

    Rg0 = Res(); Rwc = Res(); Dc1 = em.newsem("dc1")
    Rc1m = Res()
    em.op("pool", "memset", dict(ap=cst1[0:32, :], constant=0.0), writes=[Rc1m])
    Rrow = []
    for r0, src_ap, nrow in [(0, g_mix, 8), (8, g_ffn, 8), (16, g_ple, 8), (24, pool_scale, 4)]:
        rr = Res().inherit(Rc1m)
        em.dma("act", cst1[r0:r0 + nrow, :], src_ap.rearrange("(c p) -> c p", p=128), Dc1, writes=[rr])
        Rrow.append(rr)
    rr = Res()
    em.dma("act", cst2[0:12, :], w_conv.rearrange("k (c p) -> (k c) p", p=128), Dc1, writes=[rr])
    Rrow.append(rr)
    cdma(stp[0][0:120, :], spool[0:8, :, :].rearrange("s r c -> (s r) c"))
    cdma(stp[1][0:120, :], spool[8:16, :, :].rearrange("s r c -> (s r) c"))
    cdma(stc[0:32, :], sconv.rearrange("s r c -> (s r) c"))
    Rconst.w = [(Dconst, Dconst.cnt)]
    Rwg = Res(); Dwg = em.newsem("dwg")
    em.dma("pool", wg_sb[:, :, :], w_pg.rearrange("g c d -> c g d"), Dwg, writes=[Rwg])

    def mk_ident(t, r):
        em.op("pool", "memset", dict(ap=t[:, :], constant=1.0), writes=[r])
        em.op("pool", "affine_select", dict(
            out=t[:, :], in_=t[:, :], pattern=[[-1, 128]], compare_op=ALU.is_equal,
            fill=0.0, base=0, channel_multiplier=1), reads=[r], writes=[r])
    mk_ident(ident, Rident)
    mk_ident(identf, Ridentf)
    bq = 7
    em.op("pe", "transpose", dict(out=psum[:, bq, 0:32], in_=cst1[0:32, :], identity=identf[0:32, 0:32]),
          reads=Rrow + [Ridentf], writes=[Rps[bq]], sig=False)
    em.op("pe", "transpose", dict(out=psum[:, bq, 32:44], in_=cst2[0:12, :], identity=identf[0:12, 0:12]),
          reads=[Rrow[4], Ridentf], writes=[Rps[bq]])
    em.op("dve", "tensor_copy", dict(out=gTall[:, :], in_=psum[:, bq, 0:32]), reads=[Rps[bq]], writes=[Rg0])
    em.op("dve", "tensor_copy", dict(out=wconv[:, :, 0:3], in_=psum[:, bq, 32:44].rearrange("p (k c) -> p c k", k=3)),
          reads=[Rps[bq]], writes=[Rwc])
    wins = [2, 4, 8, 16]
    for g in range(4):
        em.op("pool", "memset", dict(ap=rcnt[:, g, :], constant=1.0 / wins[g]), writes=[Rrcnt])
        for t in range(wins[g] - 1):
            em.op("pool", "memset", dict(ap=rcnt[:, g, t:t + 1], constant=1.0 / (t + 1)), writes=[Rrcnt])
    for i in range(2):
        em.op("pool", "memset", dict(ap=ubuf[i][:, 0:15], constant=0.0), writes=[Ru[i]])

    statc = [0]

    def norm_front(src_ap, src_res, xn_t, xn_res, scale_eng="dve"):
        c = statc[0]
        statc[0] += 2
        ms = stat[:, c:c + 1]
        rs = stat[:, c + 1:c + 2]
        Rms = Res(); Rrs = Res()
        if not isinstance(xn_res, list):
            xn_res = [xn_res]
        em.op("act", "activation", dict(out=xn_t[:, :], in_=src_ap, func=AF.Square, scale=1.0 / 32.0, accum_out=ms),
              reads=[src_res], writes=xn_res + [Rms])
        em.op("act", "activation", dict(out=ms, in_=ms, func=AF.Sqrt, bias=EPS, scale=1.0), reads=[Rms], writes=[Rms])
        em.op("dve", "reciprocal", dict(out=rs, in_=ms), reads=[Rms], writes=[Rrs])
        if scale_eng == "dve":
            em.op("dve", "tensor_scalar", dict(out=xn_t[:, :], in0=src_ap, scalar1=rs, scalar2=None, op0=ALU.mult),
                  reads=[src_res, Rrs], writes=xn_res)
        elif scale_eng == "pool":
            em.op("pool", "tensor_tensor", dict(out=xn_t[:, :], in0=src_ap, in1=rs.to_broadcast([128, D]), op=ALU.mult),
                  reads=[src_res, Rrs], writes=xn_res)
        elif scale_eng == "act":
            em.op("act", "activation", dict(out=xn_t[:, :], in_=src_ap, func=AF.Copy, scale=rs),
                  reads=[src_res, Rrs], writes=xn_res)
        else:
            em.op("pool", "tensor_tensor", dict(out=xn_t[:, 0:512], in0=src_ap[:, 0:512], in1=rs.to_broadcast([128, 512]), op=ALU.mult),
                  reads=[src_res, Rrs], writes=xn_res[0:1])
            em.op("dve", "tensor_scalar", dict(out=xn_t[:, 512:1024], in0=src_ap[:, 512:1024], scalar1=rs, scalar2=None, op0=ALU.mult),
                  reads=[src_res, Rrs], writes=xn_res[1:2])

    def norm_back(xn_t, xn_res, gidx, t):
        if not isinstance(xn_res, list):
            xn_res = [xn_res]
        b = nb()
        for cch in range(8):
            em.op("pe", "transpose", dict(out=psum_bf[:, b, cch * 128:(cch + 1) * 128],
                                          in_=xn_t[:, cch * 128:(cch + 1) * 128], identity=ident[:, :]),
                  reads=xn_res + [Rident], writes=[Rps[b]], sig=(cch == 7))
        em.op("dve", "tensor_tensor", dict(
            out=hnT[:, :, t * 128:(t + 1) * 128],
            in0=psum_bf[:, b, :].rearrange("p (c t) -> p c t", c=8),
            in1=gT[gidx][:, :].unsqueeze(2).to_broadcast([128, 8, 128]), op=ALU.mult),
            reads=[Rps[b], Rg0], writes=[Rh[t]])

    def pipeline3(n, fa, fb, fc):
        for k in range(n + 2):
            if k < n:
                fa(k)
            if 0 <= k - 1 < n:
                fb(k - 1)
            if 0 <= k - 2 < n:
                fc(k - 2)

    def mm_fm(b, n, slot, kn, rhs_fn, rhs_res):
        for k in range(kn):
            em.op("pe", "matmul", dict(out=psum[:, b, 0:n], lhsT=wslot[slot][:, k, :], rhs=rhs_fn(k),
                                       start=(k == 0), stop=(k == kn - 1)),
                  reads=[Rw[slot]] + rhs_res, writes=[Rps[b]], sig=(k == kn - 1))
        if mm_hook[0] is not None:
            mm_hook[0]()

    mm_hook = [None]

    def mm_tm(b0, lhs_fn, lhs_res, rhs_fn, rhs_res, kn):
        for half in range(2):
            for k in range(kn):
                em.op("pe", "matmul", dict(out=psum[:, b0 + half, :], lhsT=lhs_fn(k), rhs=rhs_fn(k, half),
                                           start=(k == 0), stop=(k == kn - 1)),
                      reads=lhs_res + rhs_res, writes=[Rps[b0 + half]], sig=(k == kn - 1))

    w_issue(limit=4)

    Rv = [[Res() for _ in range(5)] for _ in range(2)]
    Rvpre = [Res(), Res()]
    Rcsb = [Res(), Res()]
    Ry = [Res() for _ in range(4)]
    Rctm = Res()
    Rv[1] = Rv[0]
    Rvpre[1] = Rvpre[0]
    Ry[2] = Ry[0]; Ry[3] = Ry[1]
    em.op("pool", "memset", dict(ap=vbuf[0][:, 0:2], constant=0.0), writes=[Rvpre[0]])

    Rxin1 = [Res() for _ in range(NX1)]; Dxin1 = [em.newsem("dxin1%d" % i) for i in range(NX1)]
    Rxn1 = [[Res(), Res()], [Res(), Res()]]
    s1k = [0]

    def s1_step():
        k = s1k[0]
        s1k[0] += 1
        for kk in (list(range(NX1 - 1)) if k == 0 else [k + NX1 - 2]):
            if kk < NT:
                em.dma("sp", xin1[kk % NX1][:, :], xtile(kk), Dxin1[kk % NX1],
                       reads=([Rxin1[(kk - 2) % NX1]] if kk >= 2 else ([Rxin1[0]] if kk == 1 else [])), writes=[Rxin1[kk % NX1]])
        if 0 <= k - 1 < NT:
            i = (k - 1) % 2
            ix = (k - 1) % NX1
            norm_front(xin1[ix][:, :], Rxin1[ix], xn1[i], Rxn1[i], scale_eng="split")
        if 0 <= k - 2 < NT:
            i = (k - 2) % 2
            norm_back(xn1[i], Rxn1[i], 0, k - 2)

    def s1_upto(t):
        while s1k[0] - 3 < t:
            s1_step()

    def us_view(buf):
        return buf[:, 15 + SEQ:LU].rearrange("p (s r) -> p s r", r=23)

    def vs_view(buf):
        return buf[:, 2 + SEQ:LV].rearrange("p (s r) -> p s r", r=10)

    def s2_front(g, before_tb=None, after_tb=None):
        ui = g % 2
        U = ubuf[ui]
        if g > 0:
            w_issue()
        slot = load_w()
        for tb in range(5):
            if before_tb is not None:
                before_tb(tb)
            n = NB[tb]
            b = nb()
            mm_fm(b, n, slot, 8, lambda k, tb=tb, n=n: hnT[:, k, CB[tb]:CB[tb] + n], [Rh[t] for t in tiles_of(tb)])
            if tb < 4:
                em.op("act", "activation", dict(out=U[:, 15 + CB[tb]:15 + CB[tb] + 512], in_=psum[:, b, :], func=AF.Copy),
                      reads=[Rps[b]], writes=[Ru[ui]])
            else:
                em.op("act", "activation", dict(out=us_view(U)[:, :, 15:23],
                                                in_=psum[:, b, 0:128].rearrange("p (s j) -> p s j", j=8), func=AF.Copy),
                      reads=[Rps[b]], writes=[Ru[ui]])
            if after_tb is not None:
                after_tb(tb)
        b = nb()
        for hh in range(2):
            em.op("pe", "transpose", dict(out=psum[:, b, hh * 120:(hh + 1) * 120],
                                          in_=stp[hh][0:120, g * 128:(g + 1) * 128], identity=identf[0:120, 0:120]),
                  reads=[Rconst, Ridentf], writes=[Rps[b]], sig=(hh == 1))
        em.op("act", "activation", dict(out=us_view(U)[:, :, 0:15],
                                        in_=psum[:, b, 0:240].rearrange("p (s r) -> p s r", r=15), func=AF.Copy),
              reads=[Rps[b]], writes=[Ru[ui]])
        b = nb()
        for j, t in enumerate([15, 16]):
            for k in range(8):
                em.op("pe", "matmul", dict(out=psum[:, b, j * 128:(j + 1) * 128], lhsT=hnT[:, k, t * 128:(t + 1) * 128],
                                           rhs=wslot[slot][:, k, :], start=(k == 0), stop=(k == 7)),
                      reads=[Rw[slot], Rh[t]], writes=[Rps[b]], sig=(k == 7 and j == 1))
        em.op("act", "activation", dict(out=ustage[:, :, g * 128:(g + 1) * 128],
                                        in_=psum[:, b, 0:256].rearrange("p (j c) -> p j c", j=2), func=AF.Copy),
              reads=[Rps[b]], writes=[Rust])
        w_done(slot)

    Rhd_p = Res()

    def s2_back_pool(g):
        ui = g % 2
        U = ubuf[ui]
        srcs = [(U, Ru[ui]), (tbuf[0], Rt[0]), (tbuf[1], Rt[1]), (tbuf[0], Rt[0]), (tbuf[1], Rt[1])]
        sh = 1
        for lvl in range(g + 1):
            src, rsrc = srcs[lvl]
            dst, rdst = srcs[lvl + 1]
            lo = 2 * sh - 1
            em.op("pool", "tensor_tensor", dict(out=dst[:, lo:15 + SEQ], in0=src[:, lo:15 + SEQ],
                                                in1=src[:, lo - sh:15 + SEQ - sh], op=ALU.add),
                  reads=[rsrc], writes=[rdst])
            em.op("pool", "tensor_tensor", dict(out=us_view(dst)[:, :, lo:23], in0=us_view(src)[:, :, lo:23],
                                                in1=us_view(src)[:, :, lo - sh:23 - sh], op=ALU.add),
                  reads=[rsrc], writes=[rdst])
            sh *= 2

    def s2_back_d(g):
        ui = g % 2
        U = ubuf[ui]
        win, rwin = [(tbuf[0], Rt[0]), (tbuf[1], Rt[1]), (tbuf[0], Rt[0]), (tbuf[1], Rt[1])][g]
        di = g % 2
        dd = dbuf[di]
        rw = rcnt[:, g, 15:16]
        em.op("pool", "tensor_tensor", dict(out=win[:, 31:15 + SEQ], in0=win[:, 31:15 + SEQ],
                                            in1=rw.to_broadcast([128, SEQ - 16]), op=ALU.mult),
              reads=[Rrcnt], writes=[rwin])
        em.op("pool", "tensor_tensor", dict(out=win[:, 15:31], in0=win[:, 15:31], in1=rcnt[:, g, :], op=ALU.mult),
              reads=[Rrcnt], writes=[rwin])
        em.op("pool", "tensor_tensor", dict(out=us_view(win)[:, :, 15:23], in0=us_view(win)[:, :, 15:23],
                                            in1=rw.unsqueeze(2).to_broadcast([128, 16, 8]), op=ALU.mult),
              reads=[Rrcnt], writes=[rwin])
        em.op("pool", "tensor_tensor", dict(out=dd[:, 0:SEQ], in0=win[:, 15:15 + SEQ], in1=U[:, 15:15 + SEQ], op=ALU.subtract),
              reads=[rwin, Ru[ui]], writes=[Rd[di]])
        em.op("pool", "tensor_tensor", dict(out=dd[:, SEQ:NTOK].rearrange("p (s j) -> p s j", j=8),
                                            in0=us_view(win)[:, :, 15:23], in1=us_view(U)[:, :, 15:23], op=ALU.subtract),
              reads=[rwin, Ru[ui]], writes=[Rd[di]])

    def s2_gmm(g):
        di = g % 2
        dd = dbuf[di]
        for tb in range(5):
            n = NB[tb]
            b = nb()
            em.op("pe", "matmul", dict(out=psum[:, b, 0:n], lhsT=wg_sb[:, g, :], rhs=dd[:, CB[tb]:CB[tb] + n],
                                       start=True, stop=True),
                  reads=[Rwg, Rd[di]], writes=[Rps[b]])
            em.op("act", "activation", dict(out=pool_out[:, g, CB[tb]:CB[tb] + n], in_=psum[:, b, 0:n], func=AF.Copy,
                                            scale=pscale[:, g:g + 1]),
                  reads=[Rps[b], Rg0], writes=[Rpo[g][tb]])


    csc = [0]

    def s3_chunk(cc):
        for _ in s3_chunk_gen(cc):
            pass

    def s3_chunk_gen(cc):
        vi = cc % 2
        V = vbuf[vi]
        if cc > 0:
            w_issue()
        sl_b = load_w(); sl_c = load_w(); sl_h = load_w()
        b = nb()
        em.op("pe", "transpose", dict(out=psum[:, b, 0:32], in_=stc[0:32, cc * 128:(cc + 1) * 128], identity=identf[0:32, 0:32]),
              reads=[Rconst, Ridentf], writes=[Rps[b]])
        em.op("act", "activation", dict(out=vs_view(V)[:, :, 0:2],
                                        in_=psum[:, b, 0:32].rearrange("p (s r) -> p s r", r=2), func=AF.Copy),
              reads=[Rps[b]], writes=[Rvpre[vi]])
        for tb in range(5):
            yield tb
            n = NB[tb]
            hres = [Rh[t] for t in tiles_of(tb)]
            rhs = lambda k, tb=tb, n=n: hnT[:, k, CB[tb]:CB[tb] + n]
            bc = nb(); mm_fm(bc, n, sl_c, 8, rhs, hres)
            bh = nb(); mm_fm(bh, n, sl_h, 8, rhs, hres)
            bb = nb(); mm_fm(bb, n, sl_b, 8, rhs, hres)
            ci = csc[0] % 2
            y0i = (csc[0] % 2) * 2
            csc[0] += 1
            cs_ = csb[ci]
            em.op("act", "activation", dict(out=cs_[:, 0:n], in_=psum[:, bc, 0:n], func=AF.Copy),
                  reads=[Rps[bc]], writes=[Rcsb[ci]])
            if tb < 4:
                c0 = CB[tb]
                em.op("dve", "tensor_tensor", dict(out=V[:, 2 + c0:2 + c0 + 512], in0=psum[:, bh, :], in1=cs_[:, :], op=ALU.mult),
                      reads=[Rps[bh], Rcsb[ci]], writes=[Rv[vi][tb]])
                vrd = [Rv[vi][tb], Rvpre[vi]] + ([Rv[vi][tb - 1]] if tb > 0 else [])
                v0 = V[:, c0:c0 + 512]; v1 = V[:, c0 + 1:c0 + 513]; v2 = V[:, c0 + 2:c0 + 514]
                ya = ybuf[y0i][:, :]; yb = ybuf[y0i + 1][:, :]
                bsrc = psum[:, bb, :]
                cdst = conv_out[:, cc, c0:c0 + 512]
            else:
                em.op("dve", "tensor_tensor", dict(out=vs_view(V)[:, :, 2:10],
                                                   in0=psum[:, bh, 0:128].rearrange("p (s j) -> p s j", j=8),
                                                   in1=cs_[:, 0:128].rearrange("p (s j) -> p s j", j=8), op=ALU.mult),
                      reads=[Rps[bh], Rcsb[ci], Rvpre[vi]], writes=[Rv[vi][tb]])
                vrd = [Rv[vi][tb], Rvpre[vi]]
                v0 = vs_view(V)[:, :, 0:8]; v1 = vs_view(V)[:, :, 1:9]; v2 = vs_view(V)[:, :, 2:10]
                ya = ybuf[y0i][:, 0:128].rearrange("p (s j) -> p s j", j=8)
                yb = ybuf[y0i + 1][:, 0:128].rearrange("p (s j) -> p s j", j=8)
                bsrc = psum[:, bb, 0:128].rearrange("p (s j) -> p s j", j=8)
                cdst = conv_out[:, cc, SEQ:NTOK].rearrange("p (s j) -> p s j", j=8)
            Rya = Ry[y0i]; Ryb = Ry[y0i + 1]
            em.op("act", "activation", dict(out=ya, in_=v0, func=AF.Copy, scale=wconv[:, cc, 0:1]),
                  reads=vrd + [Rwc], writes=[Rya])
            em.op("dve", "scalar_tensor_tensor", dict(out=yb, in0=v1, scalar=wconv[:, cc, 1:2], in1=ya, op0=ALU.mult, op1=ALU.add),
                  reads=vrd + [Rya, Rwc], writes=[Ryb])
            em.op("dve", "scalar_tensor_tensor", dict(out=ya, in0=v2, scalar=wconv[:, cc, 2:3], in1=yb, op0=ALU.mult, op1=ALU.add),
                  reads=vrd + [Ryb, Rwc], writes=[Rya])
            em.op("dve", "tensor_tensor", dict(out=cdst, in0=bsrc, in1=ya, op=ALU.mult),
                  reads=[Rps[bb], Rya], writes=[Rco[cc][tb]])
        b = nb()
        for j, t in enumerate([15, 16]):
            for wi_, sl in enumerate([sl_c, sl_h]):
                for k in range(8):
                    em.op("pe", "matmul", dict(out=psum[:, b, (2 * j + wi_) * 128:(2 * j + wi_ + 1) * 128],
                                               lhsT=hnT[:, k, t * 128:(t + 1) * 128], rhs=wslot[sl][:, k, :],
                                               start=(k == 0), stop=(k == 7)),
                          reads=[Rw[sl], Rh[t]], writes=[Rps[b]], sig=(k == 7 and j == 1 and wi_ == 1))
        pv = psum[:, b, :].rearrange("p (j w c) -> p j w c", j=2, w=2)
        em.op("act", "activation", dict(out=ctm[:, :, :], in_=pv[:, :, 0, :], func=AF.Copy), reads=[Rps[b]], writes=[Rctm])
        em.op("dve", "tensor_tensor", dict(out=vstage[:, :, cc * 128:(cc + 1) * 128], in0=pv[:, :, 1, :], in1=ctm[:, :, :], op=ALU.mult),
              reads=[Rps[b], Rctm], writes=[Rvst])
        w_done(sl_b, sl_c, sl_h)

    g3 = s3_chunk_gen(0)

    def f0_before(tb):
        s1_upto(tiles_of(tb)[-1])
        if tb == 0:
            next(g3)
        if tb == 2:
            w_issue()

    def f0_after(tb):
        try:
            next(g3)
        except StopIteration:
            pass

    def s1_hook():
        if s1k[0] < NT + 2:
            s1_step()

    s1_upto(5)
    mm_hook[0] = s1_hook
    s2_front(0, before_tb=f0_before, after_tb=f0_after)
    mm_hook[0] = None
    for g_ in range(4):
        for tb_ in range(5):
            Rpo[g_][tb_].inherit(*Rxin1[4:])
    s2_front(1)
    s2_back_pool(0); s2_back_d(0)
    s2_back_pool(1); s2_back_d(1); w_issue()
    s3_chunk(1)
    s2_front(2)
    s2_front(3)
    em.dma("sp", npp[:, :], ustage[113:128, 0, :], Dust, reads=[Rust], final=True)
    for s in range(16):
        em.dma("sp", nps[s, 7:15, :], ustage[s * 8:(s + 1) * 8, 1, :], Dust, reads=[Rust], final=True)
    s2_gmm(0)
    s2_back_pool(2); s2_back_d(2)
    s2_gmm(1)
    s2_back_pool(3); s2_back_d(3); w_issue()
    s3_chunk(2)
    s3_chunk(3)
    s2_gmm(2)
    s2_gmm(3)
    em.dma("sp", nps[:, 0:7, :], spool[:, 8:15, :], Dout, final=True)
    em.dma("sp", ncp[:, :], vstage[126:128, 0, :], Dvst, reads=[Rvst], final=True)
    for s in range(16):
        em.dma("sp", ncs[s, :, :], vstage[s * 8 + 6:s * 8 + 8, 1, :], Dvst, reads=[Rvst], final=True)

    Rs4 = [Res() for _ in range(8)]
    for r_ in Rs4:
        r_.inherit(Rt[0], Rt[1])
    for m_ in range(8):
        for tb in range(5):
            Rm[m_][tb].inherit(*Rxin1, *Rxn1[0], *Rxn1[1], Rconst, Rust, Rvst)
    em.dma("pool", wo_sb[:, :, :], w_o.rearrange("(k p) n -> p k n", p=128), Dwo, writes=[Rwo])
    s4c = [0]

    def s4_chunk(m):
        w_issue()
        sl_gp = load_w(); sl_gc = load_w(); sl_pu = load_w(); sl_co = load_w()
        for tb in range(5):
            n = NB[tb]
            c0 = CB[tb]
            hres = [Rh[t] for t in tiles_of(tb)]
            b1 = nb(); mm_fm(b1, n, sl_gp, 8, lambda k, c0=c0, n=n: hnT[:, k, c0:c0 + n], hres)
            b2 = nb(); mm_fm(b2, n, sl_pu, 4, lambda k, c0=c0, n=n: pool_out[:, k, c0:c0 + n], [Rpo[k][tb] for k in range(4)])
            b3 = nb(); mm_fm(b3, n, sl_gc, 8, lambda k, c0=c0, n=n: hnT[:, k, c0:c0 + n], hres)
            b4 = nb(); mm_fm(b4, n, sl_co, 4, lambda k, c0=c0, n=n: conv_out[:, k, c0:c0 + n], [Rco[k][tb] for k in range(4)])
            q = (s4c[0] % 2) * 4
            s4c[0] += 1
            sgp, sgc, t1, t2 = s4buf[q], s4buf[q + 1], s4buf[q + 2], s4buf[q + 3]
            em.op("act", "activation", dict(out=sgp[:, 0:n], in_=psum[:, b1, 0:n], func=AF.Sigmoid),
                  reads=[Rps[b1]], writes=[Rs4[q]])
            em.op("dve", "tensor_tensor", dict(out=t1[:, 0:n], in0=psum[:, b2, 0:n], in1=sgp[:, 0:n], op=ALU.mult),
                  reads=[Rps[b2], Rs4[q]], writes=[Rs4[q + 2]])
            em.op("act", "activation", dict(out=sgc[:, 0:n], in_=psum[:, b3, 0:n], func=AF.Sigmoid),
                  reads=[Rps[b3]], writes=[Rs4[q + 1]])
            em.op("dve", "tensor_tensor", dict(out=t2[:, 0:n], in0=psum[:, b4, 0:n], in1=sgc[:, 0:n], op=ALU.mult),
                  reads=[Rps[b4], Rs4[q + 1]], writes=[Rs4[q + 3]])
            em.op("pool", "tensor_tensor", dict(out=merged[:, m, c0:c0 + n], in0=t1[:, 0:n], in1=t2[:, 0:n], op=ALU.add),
                  reads=[Rs4[q + 2], Rs4[q + 3]], writes=[Rm[m][tb]])
        w_done(sl_gp, sl_gc, sl_pu, sl_co)

    for m in range(8):
        s4_chunk(m)

    Rxin5 = [Res(), Res()]; Dxin5 = [em.newsem("dxin5a"), em.newsem("dxin5b")]
    Rxn5 = [Res(), Res(), Res()]
    allco = [Rco[k][tb] for k in range(4) for tb in range(5)]
    for r_ in Rxin5 + Rxn5:
        r_.inherit(*allco)
    alls4 = Rs4 + [Rd[0], Rd[1], Ru[0], Ru[1]] + [Rv[0][tb] for tb in range(5)] + [Rvpre[0]] + Rcsb + Ry[0:2] + [Rctm]
    for t in range(NT):
        Rx[t].inherit(*alls4)
    allpo = [Rpo[k][tb] for k in range(4) for tb in range(5)]

    def s5_a(t):
        i = t % 2
        tb = tb_of(t)
        em.dma("sp", xin5[i][:, :], xtile(t), Dxin5[i], writes=[Rxin5[i]])
        b0 = nb2()
        mm_tm(b0, lambda k, t=t: merged[:, k, t * 128:(t + 1) * 128], [Rm[k][tb] for k in range(8)],
              lambda k, half: wo_sb[:, k, half * 512:(half + 1) * 512], [Rwo], 8)
        em.op("dve", "tensor_tensor", dict(out=x_res[:, t, :], in0=psum[:, b0:b0 + 2, :].rearrange("p a b -> p (a b)"),
                                           in1=xin5[i][:, :], op=ALU.add),
              reads=[Rps[b0], Rps[b0 + 1], Rxin5[i]], writes=[Rx[t]])

    for k in range(NT + 3):
        if k < NT:
            s5_a(k)
        if 0 <= k - 1 < NT:
            norm_front(x_res[:, k - 1, :], Rx[k - 1], xn5[(k - 1) % 3], Rxn5[(k - 1) % 3])
        if 0 <= k - 3 < NT:
            norm_back(xn5[(k - 3) % 3], Rxn5[(k - 3) % 3], 1, k - 3)

    Ract = [[Res() for _ in range(5)] for _ in range(4)]
    for fl in range(4):
        for tb in range(5):
            Ract[fl][tb].inherit(*allpo)
    Rsg6 = [Res(), Res()]; Rxn6 = [Res(), Res(), Res()]
    for r_ in Rsg6 + Rxn6:
        r_.inherit(*allco, *Rxin5, *Rxn5)
    Rwfo = [Res(), Res()]; Dwfo = [em.newsem("dwfoa"), em.newsem("dwfob")]
    allm = [Rm[k][tb] for k in range(8) for tb in range(5)]
    for r_ in Rwfo:
        r_.inherit(*allm)
    Rwpg = Res().inherit(*allm); Dwpg = em.newsem("dwpg")
    Rwple = Res().inherit(Rwo); Dwple = em.newsem("dwple")
    Rgfin = Res().inherit(Rwo); Dgfin = em.newsem("dgfin")
    sgc6 = [0]

    def s6_in(part, f0, nf, fl):
        wi = part % 2
        f = f0 + fl
        w_issue()
        sl_g = load_w(); sl_u = load_w()
        if fl == 0:
            em.dma("pool", wfo[wi][:, 0:nf, :],
                   w_ffn_out[f0 * 128:(f0 + nf) * 128, :].rearrange("(f p) n -> p f n", p=128),
                   Dwfo[wi], writes=[Rwfo[wi]])
            if part == 4:
                em.dma("pool", wpg_sb[:, :, :], w_ple_gate.rearrange("(k p) n -> p k n", p=128), Dwpg, writes=[Rwpg])
                em.dma("pool", wple_sb[:, :, :], w_ple.rearrange("(k p) n -> p k n", p=128), Dwple, writes=[Rwple])
                em.dma("sp", gfin[:, :], g_final.partition_broadcast(128), Dgfin, writes=[Rgfin])
        for tb in range(5):
            n = NB[tb]
            c0 = CB[tb]
            hres = [Rh[t] for t in tiles_of(tb)]
            bg = nb(); mm_fm(bg, n, sl_g, 8, lambda k, c0=c0, n=n: hnT[:, k, c0:c0 + n], hres)
            bu = nb(); mm_fm(bu, n, sl_u, 8, lambda k, c0=c0, n=n: hnT[:, k, c0:c0 + n], hres)
            si = sgc6[0] % 2
            sgc6[0] += 1
            em.op("act", "activation", dict(out=sg6[si][:, 0:n], in_=psum[:, bg, 0:n], func=AF.Silu),
                  reads=[Rps[bg]], writes=[Rsg6[si]])
            em.op("dve", "tensor_tensor", dict(out=act_sb[:, fl, c0:c0 + n], in0=psum[:, bu, 0:n], in1=sg6[si][:, 0:n], op=ALU.mult),
                  reads=[Rps[bu], Rsg6[si]], writes=[Ract[fl][tb]])
        w_done(sl_g, sl_u)

    def s6_out(part, nf, t):
        wi = part % 2
        tb = tb_of(t)
        b0 = nb2()
        mm_tm(b0, lambda k, t=t: act_sb[:, k, t * 128:(t + 1) * 128], [Ract[k][tb] for k in range(nf)],
              lambda k, half, wi=wi: wfo[wi][:, k, half * 512:(half + 1) * 512], [Rwfo[wi]], nf)
        em.op("dve", "tensor_tensor", dict(out=x_res[:, t, :], in0=psum[:, b0:b0 + 2, :].rearrange("p a b -> p (a b)"),
                                           in1=x_res[:, t, :], op=ALU.add),
              reads=[Rps[b0], Rps[b0 + 1]], writes=[Rx[t]])

    f0 = 0
    for part, nf in enumerate(NPART):
        for fl in range(nf):
            s6_in(part, f0, nf, fl)
        if part < len(NPART) - 1:
            for t in range(NT):
                s6_out(part, nf, t)
        else:
            def s6_norm_steps(k):
                if 0 <= k - 1 < NT:
                    norm_front(x_res[:, k - 1, :], Rx[k - 1], xn6[(k - 1) % 3], Rxn6[(k - 1) % 3], scale_eng="pool")
                if 0 <= k - 3 < NT:
                    norm_back(xn6[(k - 3) % 3], Rxn6[(k - 3) % 3], 2, k - 3)

            for k in range(NT):
                s6_out(part, nf, k)
                s6_norm_steps(k)
        f0 += nf

    allact = [Ract[fl][tb] for fl in range(4) for tb in range(5)]
    mk7 = lambda n=2: [Res().inherit(*allact, *Rsg6, *Rxn6) for _ in range(n)]
    mkwo = lambda n: [Res().inherit(Rwo, Rwple, Rgfin) for _ in range(n)]
    Rsg7 = mk7(); Rtt7 = mk7(); Rpin = mkwo(4); Rpbf = mkwo(3); RpT = mkwo(2)
    Dpin = [em.newsem("dpin%d" % i) for i in range(4)]
    Dyo = [em.newsem("dyo%d" % i) for i in range(4)]
    s7b = {}
    c7 = statc[0]
    statc[0] += 2 * NT
    GRP = 2

    def s7_pin(t):
        if t < NT:
            em.dma("sp", pin[t % 4][:, :], ptile(t), Dpin[t % 4], writes=[Rpin[t % 4]])

    def s7_cast(t):
        if t < NT:
            em.op("pool", "tensor_tensor", dict(out=pbf[t % 3][:, :], in0=pin[t % 4][:, :], in1=yst[0][:, 0:256], op=ALU.add),
                  reads=[Rpin[t % 4], Rz7], writes=[Rpbf[t % 3]])

    def s7_a(t):
        i = t % 2
        s7_pin(t + 3)
        s7_cast(t + 1)
        b = nb()
        for c in range(2):
            em.op("pe", "transpose", dict(out=psum_bf[:, b, c * 128:(c + 1) * 128],
                                          in_=pbf[t % 3][:, c * 128:(c + 1) * 128], identity=ident[:, :]),
                  reads=[Rpbf[t % 3], Rident], writes=[Rps[b]], sig=(c == 1))
        em.op("dve", "tensor_copy", dict(out=pT[i][:, :, :], in_=psum_bf[:, b, 0:256].rearrange("p (c t) -> p c t", c=2)),
              reads=[Rps[b]], writes=[RpT[i]])

    def s7_mm(t):
        i = t % 2
        bg = nb2()
        mm_tm(bg, lambda k, t=t: hnT[:, k, t * 128:(t + 1) * 128], [Rh[t]],
              lambda k, half: wpg_sb[:, k, half * 512:(half + 1) * 512], [Rwpg], 8)
        bp = nb2()
        mm_tm(bp, lambda k, i=i: pT[i][:, k, :], [RpT[i]],
              lambda k, half: wple_sb[:, k, half * 512:(half + 1) * 512], [Rwple], 2)
        s7b[t] = (bg, bp)

    def s7_b(t):
        i = t % 2
        bg, bp = s7b[t]
        em.op("act", "activation", dict(out=sg7[i][:, :], in_=psum[:, bg:bg + 2, :].rearrange("p a b -> p (a b)"), func=AF.Sigmoid),
              reads=[Rps[bg], Rps[bg + 1]], writes=[Rsg7[i]])
        em.op("dve", "tensor_tensor", dict(out=tt7[i][:, :], in0=psum[:, bp:bp + 2, :].rearrange("p a b -> p (a b)"),
                                           in1=sg7[i][:, :], op=ALU.mult),
              reads=[Rps[bp], Rps[bp + 1], Rsg7[i]], writes=[Rtt7[i]])
        em.op("pool", "tensor_tensor", dict(out=x_res[:, t, :], in0=x_res[:, t, :], in1=tt7[i][:, :], op=ALU.add),
              reads=[Rtt7[i]], writes=[Rx[t]])

    Rms7 = {}

    def s7_c(t):
        i = t % 2
        g0 = (t // GRP) * GRP
        g1 = min(g0 + GRP, NT)
        if t == g0:
            Rms7[g0] = Res()
        Rms = Rms7[g0]
        em.op("act", "activation", dict(out=tt7[i][:, :], in_=x_res[:, t, :], func=AF.Square, scale=1.0 / 32.0,
                                        accum_out=stat[:, c7 + t:c7 + t + 1]),
              reads=[Rx[t]], writes=[Rtt7[i], Rms])
        if t == g1 - 1:
            ms = stat[:, c7 + g0:c7 + g1]
            rs = stat[:, c7 + NT + g0:c7 + NT + g1]
            Rrs = Res()
            em.op("act", "activation", dict(out=ms, in_=ms, func=AF.Sqrt, bias=EPS, scale=1.0), reads=[Rms], writes=[Rms])
            em.op("dve", "reciprocal", dict(out=rs, in_=ms), reads=[Rms], writes=[Rrs])
            for tt in range(g0, g1):
                em.op("dve", "scalar_tensor_tensor", dict(out=x_res[:, tt, :], in0=x_res[:, tt, :],
                                                          scalar=stat[:, c7 + NT + tt:c7 + NT + tt + 1], in1=gfin[:, :],
                                                          op0=ALU.mult, op1=ALU.mult),
                      reads=[Rrs, Rgfin], writes=[Rx[tt]])
                em.dma("sp", ytile(tt), x_res[:, tt, :], Dyo[tt % 4], reads=[Rx[tt]], final=True)

    Rz7 = mk7(1)[0]
    em.op("pool", "memset", dict(ap=yst[0][:, 0:256], constant=0.0), writes=[Rz7])
    s7_pin(0)
    s7_pin(1)
    s7_pin(2)
    s7_cast(0)
    for k in range(NT + 2):
        if k < 3:
            s6_norm_steps(NT + k)
        if k < NT:
            s7_a(k)
        if 0 <= k - 1 < NT:
            s7_b(k - 1)
        if k < NT:
            s7_mm(k)
        if 0 <= k - 2 < NT:
            s7_c(k - 2)

    assert wst["taken"] == len(WL) and wst["issued"] == len(WL), wst
    em.finish()
    em.run()
    return nc


_NC_CACHE = {}


def kernel(x_prompt, x_sample, state_pool, state_conv, p_prompt, p_sample,
           g_mix, w_in, w_pool_group, pool_scale, w_pool_up, w_conv, w_conv_out, w_o,
           g_ffn, w_ffn_in, w_ffn_out, g_ple, w_ple, w_ple_gate, g_final):
    f = lambda a: np.ascontiguousarray(np.asarray(a, dtype=np.float32))
    x_prompt, x_sample, state_pool, state_conv, p_prompt, p_sample = map(
        f, (x_prompt, x_sample, state_pool, state_conv, p_prompt, p_sample))
    shared = dict(
        g_mix=f(g_mix)[0], w_in=f(w_in)[0], w_pg=f(w_pool_group)[0], pool_scale=f(pool_scale)[0],
        w_pool_up=f(w_pool_up)[0], w_conv=f(w_conv)[0], w_conv_out=f(w_conv_out)[0], w_o=f(w_o)[0],
        g_ffn=f(g_ffn)[0], w_ffn_in=f(w_ffn_in)[0], w_ffn_out=f(w_ffn_out)[0], g_ple=f(g_ple)[0],
        w_ple=f(w_ple)[0], w_ple_gate=f(w_ple_gate)[0], g_final=f(g_final))
    nc = build_nc()
    in_maps = []
    for c in range(8):
        m = dict(shared)
        m["xp"] = x_prompt[c]
        m["xs"] = x_sample[16 * c:16 * (c + 1)].reshape(128, D)
        m["spool"] = state_pool[0, 16 * c:16 * (c + 1)]
        m["sconv"] = state_conv[0, 16 * c:16 * (c + 1)]
        m["ppr"] = p_prompt[0, c]
        m["psa"] = p_sample[0, 16 * c:16 * (c + 1)].reshape(128, 256)
        in_maps.append(m)
    res = run_bass_kernel_spmd(nc, in_maps, core_ids=list(range(8)))
    R = res.results
    y_prompt = np.stack([np.asarray(R[c]["yp"], dtype=np.float32) for c in range(8)], axis=0)
    y_sample = np.concatenate([np.asarray(R[c]["ys"], dtype=np.float32).reshape(16, 8, D) for c in range(8)], axis=0)
    npp = np.stack([np.asarray(R[c]["npp"], dtype=np.float32) for c in range(8)], axis=0)[None]
    ncp = np.stack([np.asarray(R[c]["ncp"], dtype=np.float32) for c in range(8)], axis=0)[None]
    nps = np.concatenate([np.asarray(R[c]["nps"], dtype=np.float32) for c in range(8)], axis=0)[None]
    ncs = np.concatenate([np.asarray(R[c]["ncs"], dtype=np.float32) for c in range(8)], axis=0)[None]
    return (y_prompt, y_sample, npp, ncp, nps, ncs)
```

```python
import numpy as np
import concourse.bass as bass
import concourse.mybir as mybir
from concourse.bass_utils import run_bass_kernel_spmd

F32 = mybir.dt.float32
BF16 = mybir.dt.bfloat16
ALU = mybir.AluOpType
AF = mybir.ActivationFunctionType

D = 1024
SEQ = 2048
NT = 17
NTOK = NT * 128
NB = [512, 512, 512, 512, 128]
CB = [0, 512, 1024, 1536, 2048]
DFF = 2816
NF = 22
EPS = 1e-6
SB_BASE = 16640
SB_END = 229376


class Sem:
    def __init__(self, nc, name):
        self.h = nc.alloc_semaphore(name)
        self.cnt = 0


class Res:
    __slots__ = ("name", "w", "r")

    def __init__(self, name="r"):
        self.name = name
        self.w = []
        self.r = []

    def inherit(self, *others):
        for o in others:
            self.w = self.w + o.w + o.r
        return self


class Em:
    def __init__(self, nc):
        self.nc = nc
        self.engs = {}
        for n in ["pe", "act", "dve", "pool", "sp"]:
            self.engs[n] = dict(sem=Sem(nc, "s_" + n), seen={}, thunks=[])
        self.final = []
        self.nsem = 5

    def newsem(self, name):
        self.nsem += 1
        return Sem(self.nc, name)

    def _waits(self, e, reads, writes):
        E = self.engs[e]
        need = {}
        for r in reads:
            for (s, v) in r.w:
                if need.get(s, 0) < v:
                    need[s] = v
        for w in writes:
            for (s, v) in w.w:
                if need.get(s, 0) < v:
                    need[s] = v
            for (s, v) in w.r:
                if need.get(s, 0) < v:
                    need[s] = v
        out = []
        for s, v in need.items():
            if s is E["sem"]:
                if e == "pe":
                    continue
                if v > s.cnt:
                    continue
            if E["seen"].get(s, 0) < v:
                E["seen"][s] = v
                out.append((s, v))
        return out

    @staticmethod
    def _mark(tok, reads, writes):
        for r in reads:
            r.r = [t for t in r.r if t[0] is not tok[0]] + [tok]
        for w in writes:
            w.w = [tok]
            w.r = []

    def op(self, e, meth, kw, reads=(), writes=(), sig=True):
        E = self.engs[e]
        fn = (lambda eng, meth=meth, kw=kw: getattr(eng, meth)(**kw))
        waits = self._waits(e, reads, writes)
        s = E["sem"]
        if sig:
            s.cnt += 1
            tok = (s, s.cnt)
        else:
            tok = (s, s.cnt + 1)

        def thunk(eng, waits=waits, fn=fn, sig=sig, s=s):
            for (ws, wv) in waits:
                eng.wait_ge(ws.h, wv)
            ins = fn(eng)
            if sig:
                ins.then_inc(s.h, 1)
        E["thunks"].append(thunk)
        self._mark(tok, reads, writes)
        return tok

    def dma(self, q, out, in_, dsem, reads=(), writes=(), final=False, **kw):
        E = self.engs[q]
        waits = self._waits(q, reads, writes)
        dsem.cnt += 16
        tok = (dsem, dsem.cnt)

        def thunk(eng, waits=waits, out=out, in_=in_, kw=kw, dsem=dsem):
            for (ws, wv) in waits:
                eng.wait_ge(ws.h, wv)
            eng.dma_start(out=out, in_=in_, **kw).then_inc(dsem.h, 16)
        E["thunks"].append(thunk)
        self._mark(tok, reads, writes)
        if final:
            self.final.append(tok)
        return tok

    def finish(self):
        need = {}
        for (s, v) in self.final:
            need[s] = max(need.get(s, 0), v)
        lst = list(need.items())

        def thunk(eng):
            for (s, v) in lst:
                eng.wait_ge(s.h, v)
        self.engs["sp"]["thunks"].append(thunk)

    def run(self):
        nc = self.nc
        for n, E in self.engs.items():
            assert E["sem"].cnt < 60000, (n, E["sem"].cnt)
        with nc.Block() as block:
            @block.tensor
            def _(eng):
                for t in self.engs["pe"]["thunks"]:
                    t(eng)

            @block.scalar
            def _(eng):
                for t in self.engs["act"]["thunks"]:
                    t(eng)

            @block.vector
            def _(eng):
                for t in self.engs["dve"]["thunks"]:
                    t(eng)

            @block.gpsimd
            def _(eng):
                for t in self.engs["pool"]["thunks"]:
                    t(eng)

            @block.sync
            def _(eng):
                for t in self.engs["sp"]["thunks"]:
                    t(eng)


def build_nc():
    nc = bass.Bass("TRN2", target_bir_lowering=False)

    def din(name, shape):
        return nc.dram_tensor(name, list(shape), F32, kind="ExternalInput").ap()

    def dout(name, shape):
        return nc.dram_tensor(name, list(shape), F32, kind="ExternalOutput").ap()

    xp = din("xp", [SEQ, D]); xs = din("xs", [128, D])
    spool = din("spool", [16, 15, 512]); sconv = din("sconv", [16, 2, 512])
    ppr = din("ppr", [SEQ, 256]); psa = din("psa", [128, 256])
    g_mix = din("g_mix", [D]); w_in = din("w_in", [D, 4096])
    w_pg = din("w_pg", [4, 128, 128]); pool_scale = din("pool_scale", [512])
    w_pool_up = din("w_pool_up", [512, D]); w_conv = din("w_conv", [3, 512])
    w_conv_out = din("w_conv_out", [512, D]); w_o = din("w_o", [D, D])
    g_ffn = din("g_ffn", [D]); w_ffn_in = din("w_ffn_in", [D, 2 * DFF])
    w_ffn_out = din("w_ffn_out", [DFF, D]); g_ple = din("g_ple", [D])
    w_ple = din("w_ple", [256, D]); w_ple_gate = din("w_ple_gate", [D, D])
    g_final = din("g_final", [D])

    yp = dout("yp", [SEQ, D]); ys = dout("ys", [128, D])
    npp = dout("npp", [15, 512]); ncp = dout("ncp", [2, 512])
    nps = dout("nps", [16, 15, 512]); ncs = dout("ncs", [16, 2, 512])

    em = Em(nc)

    def xtile(t):
        return xp[t * 128:(t + 1) * 128, :] if t < 16 else xs[:, :]

    def ytile(t):
        return yp[t * 128:(t + 1) * 128, :] if t < 16 else ys[:, :]

    def ptile(t):
        return ppr[t * 128:(t + 1) * 128, :] if t < 16 else psa[:, :]

    cnt = [0]

    def sbt(off, shape, dt):
        nbytes = int(np.prod(shape[1:])) * (4 if dt == F32 else 2)
        assert off % 32 == 0, off
        assert SB_BASE <= off and off + nbytes <= SB_END, (off, nbytes)
        cnt[0] += 1
        return nc.alloc_sbuf_tensor_at("t%d" % cnt[0], list(shape), dt, offset=off), off + nbytes

    A0 = SB_BASE
    B0 = A0 + NT * 4096
    C0 = B0 + 8 * NTOK * 2
    D0 = C0 + 69632
    assert D0 + 38656 <= SB_END

    x_res, _ = sbt(A0, [128, NT, D], F32)
    hnT, _ = sbt(B0, [128, 8, NTOK], BF16)
    pool_out, _ = sbt(C0, [128, 4, NTOK], BF16)
    conv_out, _ = sbt(C0 + 17408, [128, 4, NTOK], BF16)
    merged, _ = sbt(C0 + 34816, [128, 8, NTOK], BF16)
    NSLOT = 8
    wslot = []
    o = D0
    for i in range(NSLOT):
        t_, o = sbt(o, [128, 8, 128], BF16)
        wslot.append(t_)
    wo_sb, o = sbt(o, [128, 8, D], BF16)
    WO_OFF = o - 16384
    ident, o = sbt(o, [128, 128], BF16)
    identf, o = sbt(o, [128, 128], F32)
    gTall, o = sbt(o, [128, 32], F32)
    gT = [gTall[:, 8 * i:8 * i + 8] for i in range(3)]
    pscale = gTall[:, 24:32]
    cst1, o = sbt(o, [128, 128], F32)
    cst2, o = sbt(o, [128, 128], F32)
    wconv, o = sbt(o, [128, 4, 4], F32)
    wg_sb, o = sbt(o, [128, 4, 128], BF16)
    rcnt, o = sbt(o, [128, 4, 16], F32)
    stat, o = sbt(o, [128, 4 * NT * 2], F32)
    hdbuf, o = sbt(o, [128, 16], F32)
    assert o <= SB_END, o

    LU = 15 + SEQ + 16 * 23
    UB = 9728
    o = A0
    ubuf = []
    for i in range(2):
        t_, o2 = sbt(o, [128, LU], F32)
        ubuf.append(t_); o += UB
    tbuf = []
    for i in range(2):
        t_, o2 = sbt(o, [128, LU], F32)
        tbuf.append(t_); o += UB
    dbuf = []
    for i in range(2):
        t_, o = sbt(o, [128, NTOK], BF16)
        dbuf.append(t_)
    LV = 2 + SEQ + 16 * 10
    vb_, o = sbt(o, [128, LV], F32)
    o = (o + 31) // 32 * 32
    vbuf = [vb_, vb_]
    csb = []
    for i in range(2):
        t_, o = sbt(o, [128, 512], F32)
        csb.append(t_)
    ybuf = []
    for i in range(2):
        t_, o = sbt(o, [128, 512], F32)
        ybuf.append(t_)
    ybuf = ybuf + ybuf
    ctm, o = sbt(o, [128, 2, 128], F32)
    assert o <= B0, o
    o = A0 + 2 * UB
    s4buf = []
    for i in range(8):
        t_, o = sbt(o, [128, 512], F32)
        s4buf.append(t_)
    assert o <= A0 + 4 * UB
    o = C0 + 34816
    NX1 = 8
    xin1 = []
    for i in range(4):
        t_, o = sbt(o, [128, D], F32)
        xin1.append(t_)
    for i in range(4):
        t_, _ = sbt(C0 + i * 4096, [128, D], F32)
        xin1.append(t_)
    xn1 = []
    for i in range(2):
        t_, o = sbt(o, [128, D], BF16)
        xn1.append(t_)
    stp = []
    for i in range(2):
        t_, o = sbt(o, [128, 512], F32)
        stp.append(t_)
    stc, o = sbt(o, [128, 512], F32)
    ustage, o = sbt(o, [128, 2, 512], F32)
    vstage, o = sbt(o, [128, 2, 512], F32)
    assert o <= C0 + 69632, o
    o = C0 + 17408
    xin5 = []
    for i in range(2):
        t_, o = sbt(o, [128, D], F32)
        xin5.append(t_)
    xn5 = []
    for i in range(3):
        t_, o = sbt(o, [128, D], BF16)
        xn5.append(t_)
    assert o <= C0 + 34816
    NPART = [3, 3, 4, 4, 4, 4]
    act_sb, _ = sbt(C0, [128, 4, NTOK], BF16)
    o = C0 + 17408
    sg6 = []
    for i in range(2):
        t_, o = sbt(o, [128, 512], F32)
        sg6.append(t_)
    xn6 = []
    for i in range(3):
        t_, o = sbt(o, [128, D], BF16)
        xn6.append(t_)
    assert o <= C0 + 34816
    o = C0 + 34816
    wfo = []
    for i in range(2):
        t_, o = sbt(o, [128, 4, D], BF16)
        wfo.append(t_)
    wpg_sb, o = sbt(o, [128, 8, D], BF16)
    assert o <= D0
    o = C0
    sg7 = []
    for i in range(2):
        t_, o = sbt(o, [128, D], F32)
        sg7.append(t_)
    tt7 = []
    for i in range(2):
        t_, o = sbt(o, [128, D], F32)
        tt7.append(t_)
    yst = []
    for i in range(2):
        t_, o = sbt(o, [128, D], F32)
        yst.append(t_)
    assert o <= C0 + 34816
    wple_sb, o = sbt(WO_OFF, [128, 2, D], BF16)
    gfin, o = sbt(o, [128, D], F32)
    pin = []
    for i in range(4):
        t_, o = sbt(o, [128, 256], F32)
        pin.append(t_)
    pbf = []
    for i in range(3):
        t_, o = sbt(o, [128, 256], BF16)
        pbf.append(t_)
    pT = []
    for i in range(2):
        t_, o = sbt(o, [128, 2, 128], BF16)
        pT.append(t_)
    assert o <= WO_OFF + 16384

    psum = nc.alloc_psum_tensor("psum", [128, 8, 512], F32)
    psum_bf = psum.bitcast(BF16)

    Rps = [Res("ps%d" % i) for i in range(8)]
    Rh = [Res() for _ in range(NT)]
    Rx = [Res() for _ in range(NT)]
    Rpo = [[Res() for _ in range(5)] for _ in range(4)]
    Rco = [[Res() for _ in range(5)] for _ in range(4)]
    Rm = [[Res() for _ in range(5)] for _ in range(8)]
    Rw = [Res() for _ in range(NSLOT)]
    Dw = [em.newsem("dw%d" % i) for i in range(NSLOT)]
    Rwo = Res(); Dwo = em.newsem("dwo")
    Rconst = Res(); Dconst = em.newsem("dconst")
    Rident = Res(); Ridentf = Res(); Rrcnt = Res()
    Ru = [Res(), Res()]; Rt = [Res(), Res()]; Rd = [Res(), Res()]
    Rust = Res(); Rvst = Res(); Dust = em.newsem("dust"); Dvst = em.newsem("dvst")
    Dout = em.newsem("dout")
    Dmisc = em.newsem("dmisc")

    bankc = [0]

    def nb():
        b = bankc[0] % 8
        bankc[0] += 1
        return b

    def nb2():
        if bankc[0] % 2:
            bankc[0] += 1
        b = bankc[0] % 8
        bankc[0] += 2
        return b

    wcol = lambda src, c0: src[:, c0:c0 + 128].rearrange("(k p) n -> p k n", p=128)
    WL = []
    def wl_s3(cc):
        WL.append((wcol(w_in, 512 + cc * 128), 8))
        WL.append((wcol(w_in, 1024 + cc * 128), 8))
        WL.append((wcol(w_in, 1536 + cc * 128), 8))
    wl_u = lambda g: WL.append((wcol(w_in, g * 128), 8))
    wl_u(0); wl_s3(0); wl_u(1); wl_s3(1); wl_u(2); wl_u(3); wl_s3(2); wl_s3(3)
    for m_ in range(8):
        WL.append((wcol(w_in, 2048 + m_ * 128), 8))
        WL.append((wcol(w_in, 3072 + m_ * 128), 8))
        WL.append((wcol(w_pool_up, m_ * 128), 4))
        WL.append((wcol(w_conv_out, m_ * 128), 4))
    for f_ in range(NF):
        WL.append((wcol(w_ffn_in, f_ * 128), 8))
        WL.append((wcol(w_ffn_in, DFF + f_ * 128), 8))
    wst = dict(issued=0, taken=0, done=0)

    def w_issue(limit=None):
        while wst["issued"] < len(WL) and wst["issued"] < wst["done"] + NSLOT and (limit is None or wst["issued"] < limit):
            j = wst["issued"]
            src_ap, kn = WL[j]
            em.dma("pool", wslot[j % NSLOT][:, 0:kn, :], src_ap, Dw[j % NSLOT], writes=[Rw[j % NSLOT]])
            wst["issued"] += 1

    slot_idx = {}
    wflags = [False] * len(WL)

    def load_w():
        j = wst["taken"]
        wst["taken"] += 1
        assert j < wst["issued"], (j, wst)
        slot_idx[j % NSLOT] = j
        return j % NSLOT

    def w_done(*slots):
        for sl in slots:
            wflags[slot_idx[sl]] = True
        while wst["done"] < len(WL) and wflags[wst["done"]]:
            wst["done"] += 1

    def tiles_of(tb):
        return list(range(4 * tb, 4 * tb + 4)) if tb < 4 else [16]

    def tb_of(t):
        return t // 4 if t < 16 else 4

    def cdma(out, in_, **kw):
        em.dma("act", out, in_, Dconst, writes=[Rconst], **kw)
        Rconst.w = []

    Rg0 = Res(); Rwc = Res(); Dc1 = em.newsem("dc1")
    Rc1m = Res()
    em.op("pool", "memset", dict(ap=cst1[0:32, :], constant=0.0), writes=[Rc1m])
    Rrow = []
    for r0, src_ap, nrow in [(0, g_mix, 8), (8, g_ffn, 8), (16, g_ple, 8), (24, pool_scale, 4)]:
        rr = Res().inherit(Rc1m)
        em.dma("act", cst1[r0:r0 + nrow, :], src_ap.rearrange("(c p) -> c p", p=128), Dc1, writes=[rr])
        Rrow.append(rr)
    rr = Res()
    em.dma("act", cst2[0:12, :], w_conv.rearrange("k (c p) -> (k c) p", p=128), Dc1, writes=[rr])
    Rrow.append(rr)
    cdma(stp[0][0:120, :], spool[0:8, :, :].rearrange("s r c -> (s r) c"))
    cdma(stp[1][0:120, :], spool[8:16, :, :].rearrange("s r c -> (s r) c"))
    cdma(stc[0:32, :], sconv.rearrange("s r c -> (s r) c"))
    Rconst.w = [(Dconst, Dconst.cnt)]
    Rwg = Res(); Dwg = em.newsem("dwg")
    em.dma("pool", wg_sb[:, :, :], w_pg.rearrange("g c d -> c g d"), Dwg, writes=[Rwg])

    def mk_ident(t, r):
        em.op("pool", "memset", dict(ap=t[:, :], constant=1.0), writes=[r])
        em.op("pool", "affine_select", dict(
            out=t[:, :], in_=t[:, :], pattern=[[-1, 128]], compare_op=ALU.is_equal,
            fill=0.0, base=0, channel_multiplier=1), reads=[r], writes=[r])
    mk_ident(ident, Rident)
    mk_ident(identf, Ridentf)
    bq = 7
    em.op("pe", "transpose", dict(out=psum[:, bq, 0:32], in_=cst1[0:32, :], identity=identf[0:32, 0:32]),
          reads=Rrow + [Ridentf], writes=[Rps[bq]], sig=False)
    em.op("pe", "transpose", dict(out=psum[:, bq, 32:44], in_=cst2[0:12, :], identity=identf[0:12, 0:12]),
          reads=[Rrow[4], Ridentf], writes=[Rps[bq]])
    em.op("dve", "tensor_copy", dict(out=gTall[:, :], in_=psum[:, bq, 0:32]), reads=[Rps[bq]], writes=[Rg0])
    em.op("dve", "tensor_copy", dict(out=wconv[:, :, 0:3], in_=psum[:, bq, 32:44].rearrange("p (k c) -> p c k", k=3)),
          reads=[Rps[bq]], writes=[Rwc])
    wins = [2, 4, 8, 16]
    for g in range(4):
        em.op("pool", "memset", dict(ap=rcnt[:, g, :], constant=1.0 / wins[g]), writes=[Rrcnt])
        for t in range(wins[g] - 1):
            em.op("pool", "memset", dict(ap=rcnt[:, g, t:t + 1], constant=1.0 / (t + 1)), writes=[Rrcnt])
    for i in range(2):
        em.op("pool", "memset", dict(ap=ubuf[i][:, 0:15], constant=0.0), writes=[Ru[i]])

    statc = [0]

    def norm_front(src_ap, src_res, xn_t, xn_res, scale_eng="dve"):
        c = statc[0]
        statc[0] += 2
        ms = stat[:, c:c + 1]
        rs = stat[:, c + 1:c + 2]
        Rms = Res(); Rrs = Res()
        if not isinstance(xn_res, list):
            xn_res = [xn_res]
        em.op("act", "activation", dict(out=xn_t[:, :], in_=src_ap, func=AF.Square, scale=1.0 / 32.0, accum_out=ms),
              reads=[src_res], writes=xn_res + [Rms])
        em.op("act", "activation", dict(out=ms, in_=ms, func=AF.Sqrt, bias=EPS, scale=1.0), reads=[Rms], writes=[Rms])
        em.op("dve", "reciprocal", dict(out=rs, in_=ms), reads=[Rms], writes=[Rrs])
        if scale_eng == "dve":
            em.op("dve", "tensor_scalar", dict(out=xn_t[:, :], in0=src_ap, scalar1=rs, scalar2=None, op0=ALU.mult),
                  reads=[src_res, Rrs], writes=xn_res)
        elif scale_eng == "pool":
            em.op("pool", "tensor_tensor", dict(out=xn_t[:, :], in0=src_ap, in1=rs.to_broadcast([128, D]), op=ALU.mult),
                  reads=[src_res, Rrs], writes=xn_res)
        elif scale_eng == "act":
            em.op("act", "activation", dict(out=xn_t[:, :], in_=src_ap, func=AF.Copy, scale=rs),
                  reads=[src_res, Rrs], writes=xn_res)
        else:
            em.op("pool", "tensor_tensor", dict(out=xn_t[:, 0:512], in0=src_ap[:, 0:512], in1=rs.to_broadcast([128, 512]), op=ALU.mult),
                  reads=[src_res, Rrs], writes=xn_res[0:1])
            em.op("dve", "tensor_scalar", dict(out=xn_t[:, 512:1024], in0=src_ap[:, 512:1024], scalar1=rs, scalar2=None, op0=ALU.mult),
                  reads=[src_res, Rrs], writes=xn_res[1:2])

    def norm_back(xn_t, xn_res, gidx, t):
        if not isinstance(xn_res, list):
            xn_res = [xn_res]
        b = nb()
        for cch in range(8):
            em.op("pe", "transpose", dict(out=psum_bf[:, b, cch * 128:(cch + 1) * 128],
                                          in_=xn_t[:, cch * 128:(cch + 1) * 128], identity=ident[:, :]),
                  reads=xn_res + [Rident], writes=[Rps[b]], sig=(cch == 7))
        if gidx == 2:
            em.op("dve", "tensor_copy", dict(out=hnT[:, :, t * 128:(t + 1) * 128],
                                             in_=psum_bf[:, b, :].rearrange("p (c t) -> p c t", c=8)),
                  reads=[Rps[b]], writes=[Rh[t]])
            return
        em.op("dve", "tensor_tensor", dict(
            out=hnT[:, :, t * 128:(t + 1) * 128],
            in0=psum_bf[:, b, :].rearrange("p (c t) -> p c t", c=8),
            in1=gT[gidx][:, :].unsqueeze(2).to_broadcast([128, 8, 128]), op=ALU.mult),
            reads=[Rps[b], Rg0], writes=[Rh[t]])

    def pipeline3(n, fa, fb, fc):
        for k in range(n + 2):
            if k < n:
                fa(k)
            if 0 <= k - 1 < n:
                fb(k - 1)
            if 0 <= k - 2 < n:
                fc(k - 2)

    def mm_fm(b, n, slot, kn, rhs_fn, rhs_res):
        for k in range(kn):
            em.op("pe", "matmul", dict(out=psum[:, b, 0:n], lhsT=wslot[slot][:, k, :], rhs=rhs_fn(k),
                                       start=(k == 0), stop=(k == kn - 1)),
                  reads=[Rw[slot]] + rhs_res, writes=[Rps[b]], sig=(k == kn - 1))
        if mm_hook[0] is not None:
            mm_hook[0]()

    mm_hook = [None]

    def mm_tm(b0, lhs_fn, lhs_res, rhs_fn, rhs_res, kn):
        for half in range(2):
            for k in range(kn):
                em.op("pe", "matmul", dict(out=psum[:, b0 + half, :], lhsT=lhs_fn(k), rhs=rhs_fn(k, half),
                                           start=(k == 0), stop=(k == kn - 1)),
                      reads=lhs_res + rhs_res, writes=[Rps[b0 + half]], sig=(k == kn - 1))

    w_issue(limit=4)

    Rv = [[Res() for _ in range(5)] for _ in range(2)]
    Rvpre = [Res(), Res()]
    Rcsb = [Res(), Res()]
    Ry = [Res() for _ in range(4)]
    Rctm = Res()
    Rv[1] = Rv[0]
    Rvpre[1] = Rvpre[0]
    Ry[2] = Ry[0]; Ry[3] = Ry[1]
    em.op("pool", "memset", dict(ap=vbuf[0][:, 0:2], constant=0.0), writes=[Rvpre[0]])

    Rxin1 = [Res() for _ in range(NX1)]; Dxin1 = [em.newsem("dxin1%d" % i) for i in range(NX1)]
    Rxn1 = [[Res(), Res()], [Res(), Res()]]
    s1k = [0]

    def s1_step():
        k = s1k[0]
        s1k[0] += 1
        for kk in (list(range(NX1 - 1)) if k == 0 else [k + NX1 - 2]):
            if kk < NT:
                em.dma("sp", xin1[kk % NX1][:, :], xtile(kk), Dxin1[kk % NX1],
                       reads=([Rxin1[(kk - 2) % NX1]] if kk >= 2 else ([Rxin1[0]] if kk == 1 else [])), writes=[Rxin1[kk % NX1]])
        if 0 <= k - 1 < NT:
            i = (k - 1) % 2
            ix = (k - 1) % NX1
            norm_front(xin1[ix][:, :], Rxin1[ix], xn1[i], Rxn1[i], scale_eng="split")
        if 0 <= k - 2 < NT:
            i = (k - 2) % 2
            norm_back(xn1[i], Rxn1[i], 0, k - 2)

    def s1_upto(t):
        while s1k[0] - 3 < t:
            s1_step()

    def us_view(buf):
        return buf[:, 15 + SEQ:LU].rearrange("p (s r) -> p s r", r=23)

    def vs_view(buf):
        return buf[:, 2 + SEQ:LV].rearrange("p (s r) -> p s r", r=10)

    def s2_front(g, before_tb=None, after_tb=None):
        ui = g % 2
        U = ubuf[ui]
        if g > 0:
            w_issue()
        slot = load_w()
        for tb in range(5):
            if before_tb is not None:
                before_tb(tb)
            n = NB[tb]
            b = nb()
            mm_fm(b, n, slot, 8, lambda k, tb=tb, n=n: hnT[:, k, CB[tb]:CB[tb] + n], [Rh[t] for t in tiles_of(tb)])
            if tb < 4:
                em.op("act", "activation", dict(out=U[:, 15 + CB[tb]:15 + CB[tb] + 512], in_=psum[:, b, :], func=AF.Copy),
                      reads=[Rps[b]], writes=[Ru[ui]])
            else:
                em.op("act", "activation", dict(out=us_view(U)[:, :, 15:23],
                                                in_=psum[:, b, 0:128].rearrange("p (s j) -> p s j", j=8), func=AF.Copy),
                      reads=[Rps[b]], writes=[Ru[ui]])
            if after_tb is not None:
                after_tb(tb)
        b = nb()
        for hh in range(2):
            em.op("pe", "transpose", dict(out=psum[:, b, hh * 120:(hh + 1) * 120],
                                          in_=stp[hh][0:120, g * 128:(g + 1) * 128], identity=identf[0:120, 0:120]),
                  reads=[Rconst, Ridentf], writes=[Rps[b]], sig=(hh == 1))
        em.op("act", "activation", dict(out=us_view(U)[:, :, 0:15],
                                        in_=psum[:, b, 0:240].rearrange("p (s r) -> p s r", r=15), func=AF.Copy),
              reads=[Rps[b]], writes=[Ru[ui]])
        b = nb()
        for j, t in enumerate([15, 16]):
            for k in range(8):
                em.op("pe", "matmul", dict(out=psum[:, b, j * 128:(j + 1) * 128], lhsT=hnT[:, k, t * 128:(t + 1) * 128],
                                           rhs=wslot[slot][:, k, :], start=(k == 0), stop=(k == 7)),
                      reads=[Rw[slot], Rh[t]], writes=[Rps[b]], sig=(k == 7 and j == 1))
        em.op("act", "activation", dict(out=ustage[:, :, g * 128:(g + 1) * 128],
                                        in_=psum[:, b, 0:256].rearrange("p (j c) -> p j c", j=2), func=AF.Copy),
              reads=[Rps[b]], writes=[Rust])
        w_done(slot)

    Rhd_p = Res()

    def s2_back_pool(g):
        ui = g % 2
        U = ubuf[ui]
        srcs = [(U, Ru[ui]), (tbuf[0], Rt[0]), (tbuf[1], Rt[1]), (tbuf[0], Rt[0]), (tbuf[1], Rt[1])]
        sh = 1
        for lvl in range(g + 1):
            src, rsrc = srcs[lvl]
            dst, rdst = srcs[lvl + 1]
            lo = 2 * sh - 1
            em.op("pool", "tensor_tensor", dict(out=dst[:, lo:15 + SEQ], in0=src[:, lo:15 + SEQ],
                                                in1=src[:, lo - sh:15 + SEQ - sh], op=ALU.add),
                  reads=[rsrc], writes=[rdst])
            em.op("pool", "tensor_tensor", dict(out=us_view(dst)[:, :, lo:23], in0=us_view(src)[:, :, lo:23],
                                                in1=us_view(src)[:, :, lo - sh:23 - sh], op=ALU.add),
                  reads=[rsrc], writes=[rdst])
            sh *= 2

    def s2_back_d(g):
        ui = g % 2
        U = ubuf[ui]
        win, rwin = [(tbuf[0], Rt[0]), (tbuf[1], Rt[1]), (tbuf[0], Rt[0]), (tbuf[1], Rt[1])][g]
        di = g % 2
        dd = dbuf[di]
        rw = rcnt[:, g, 15:16]
        em.op("pool", "tensor_tensor", dict(out=win[:, 31:15 + SEQ], in0=win[:, 31:15 + SEQ],
                                            in1=rw.to_broadcast([128, SEQ - 16]), op=ALU.mult),
              reads=[Rrcnt], writes=[rwin])
        em.op("pool", "tensor_tensor", dict(out=win[:, 15:31], in0=win[:, 15:31], in1=rcnt[:, g, :], op=ALU.mult),
              reads=[Rrcnt], writes=[rwin])
        em.op("pool", "tensor_tensor", dict(out=us_view(win)[:, :, 15:23], in0=us_view(win)[:, :, 15:23],
                                            in1=rw.unsqueeze(2).to_broadcast([128, 16, 8]), op=ALU.mult),
              reads=[Rrcnt], writes=[rwin])
        em.op("pool", "tensor_tensor", dict(out=dd[:, 0:SEQ], in0=win[:, 15:15 + SEQ], in1=U[:, 15:15 + SEQ], op=ALU.subtract),
              reads=[rwin, Ru[ui]], writes=[Rd[di]])
        em.op("pool", "tensor_tensor", dict(out=dd[:, SEQ:NTOK].rearrange("p (s j) -> p s j", j=8),
                                            in0=us_view(win)[:, :, 15:23], in1=us_view(U)[:, :, 15:23], op=ALU.subtract),
              reads=[rwin, Ru[ui]], writes=[Rd[di]])

    def s2_gmm(g):
        di = g % 2
        dd = dbuf[di]
        for tb in range(5):
            n = NB[tb]
            b = nb()
            em.op("pe", "matmul", dict(out=psum[:, b, 0:n], lhsT=wg_sb[:, g, :], rhs=dd[:, CB[tb]:CB[tb] + n],
                                       start=True, stop=True),
                  reads=[Rwg, Rd[di]], writes=[Rps[b]])
            em.op("act", "activation", dict(out=pool_out[:, g, CB[tb]:CB[tb] + n], in_=psum[:, b, 0:n], func=AF.Copy,
                                            scale=pscale[:, g:g + 1]),
                  reads=[Rps[b], Rg0], writes=[Rpo[g][tb]])


    csc = [0]

    def s3_chunk(cc):
        for _ in s3_chunk_gen(cc):
            pass

    def s3_chunk_gen(cc):
        vi = cc % 2
        V = vbuf[vi]
        if cc > 0:
            w_issue()
        sl_b = load_w(); sl_c = load_w(); sl_h = load_w()
        b = nb()
        em.op("pe", "transpose", dict(out=psum[:, b, 0:32], in_=stc[0:32, cc * 128:(cc + 1) * 128], identity=identf[0:32, 0:32]),
              reads=[Rconst, Ridentf], writes=[Rps[b]])
        em.op("act", "activation", dict(out=vs_view(V)[:, :, 0:2],
                                        in_=psum[:, b, 0:32].rearrange("p (s r) -> p s r", r=2), func=AF.Copy),
              reads=[Rps[b]], writes=[Rvpre[vi]])
        for tb in range(5):
            yield tb
            n = NB[tb]
            hres = [Rh[t] for t in tiles_of(tb)]
            rhs = lambda k, tb=tb, n=n: hnT[:, k, CB[tb]:CB[tb] + n]
            bc = nb(); mm_fm(bc, n, sl_c, 8, rhs, hres)
            bh = nb(); mm_fm(bh, n, sl_h, 8, rhs, hres)
            bb = nb(); mm_fm(bb, n, sl_b, 8, rhs, hres)
            ci = csc[0] % 2
            y0i = (csc[0] % 2) * 2
            csc[0] += 1
            cs_ = csb[ci]
            em.op("act", "activation", dict(out=cs_[:, 0:n], in_=psum[:, bc, 0:n], func=AF.Copy),
                  reads=[Rps[bc]], writes=[Rcsb[ci]])
            if tb < 4:
                c0 = CB[tb]
                em.op("dve", "tensor_tensor", dict(out=V[:, 2 + c0:2 + c0 + 512], in0=psum[:, bh, :], in1=cs_[:, :], op=ALU.mult),
                      reads=[Rps[bh], Rcsb[ci]], writes=[Rv[vi][tb]])
                vrd = [Rv[vi][tb], Rvpre[vi]] + ([Rv[vi][tb - 1]] if tb > 0 else [])
                v0 = V[:, c0:c0 + 512]; v1 = V[:, c0 + 1:c0 + 513]; v2 = V[:, c0 + 2:c0 + 514]
                ya = ybuf[y0i][:, :]; yb = ybuf[y0i + 1][:, :]
                bsrc = psum[:, bb, :]
                cdst = conv_out[:, cc, c0:c0 + 512]
            else:
                em.op("dve", "tensor_tensor", dict(out=vs_view(V)[:, :, 2:10],
                                                   in0=psum[:, bh, 0:128].rearrange("p (s j) -> p s j", j=8),
                                                   in1=cs_[:, 0:128].rearrange("p (s j) -> p s j", j=8), op=ALU.mult),
                      reads=[Rps[bh], Rcsb[ci], Rvpre[vi]], writes=[Rv[vi][tb]])
                vrd = [Rv[vi][tb], Rvpre[vi]]
                v0 = vs_view(V)[:, :, 0:8]; v1 = vs_view(V)[:, :, 1:9]; v2 = vs_view(V)[:, :, 2:10]
                ya = ybuf[y0i][:, 0:128].rearrange("p (s j) -> p s j", j=8)
                yb = ybuf[y0i + 1][:, 0:128].rearrange("p (s j) -> p s j", j=8)
                bsrc = psum[:, bb, 0:128].rearrange("p (s j) -> p s j", j=8)
                cdst = conv_out[:, cc, SEQ:NTOK].rearrange("p (s j) -> p s j", j=8)
            Rya = Ry[y0i]; Ryb = Ry[y0i + 1]
            em.op("act", "activation", dict(out=ya, in_=v0, func=AF.Copy, scale=wconv[:, cc, 0:1]),
                  reads=vrd + [Rwc], writes=[Rya])
            em.op("dve", "scalar_tensor_tensor", dict(out=yb, in0=v1, scalar=wconv[:, cc, 1:2], in1=ya, op0=ALU.mult, op1=ALU.add),
                  reads=vrd + [Rya, Rwc], writes=[Ryb])
            em.op("dve", "scalar_tensor_tensor", dict(out=ya, in0=v2, scalar=wconv[:, cc, 2:3], in1=yb, op0=ALU.mult, op1=ALU.add),
                  reads=vrd + [Ryb, Rwc], writes=[Rya])
            em.op("dve", "tensor_tensor", dict(out=cdst, in0=bsrc, in1=ya, op=ALU.mult),
                  reads=[Rps[bb], Rya], writes=[Rco[cc][tb]])
        b = nb()
        for j, t in enumerate([15, 16]):
            for wi_, sl in enumerate([sl_c, sl_h]):
                for k in range(8):
                    em.op("pe", "matmul", dict(out=psum[:, b, (2 * j + wi_) * 128:(2 * j + wi_ + 1) * 128],
                                               lhsT=hnT[:, k, t * 128:(t + 1) * 128], rhs=wslot[sl][:, k, :],
                                               start=(k == 0), stop=(k == 7)),
                          reads=[Rw[sl], Rh[t]], writes=[Rps[b]], sig=(k == 7 and j == 1 and wi_ == 1))
        pv = psum[:, b, :].rearrange("p (j w c) -> p j w c", j=2, w=2)
        em.op("act", "activation", dict(out=ctm[:, :, :], in_=pv[:, :, 0, :], func=AF.Copy), reads=[Rps[b]], writes=[Rctm])
        em.op("dve", "tensor_tensor", dict(out=vstage[:, :, cc * 128:(cc + 1) * 128], in0=pv[:, :, 1, :], in1=ctm[:, :, :], op=ALU.mult),
              reads=[Rps[b], Rctm], writes=[Rvst])
        w_done(sl_b, sl_c, sl_h)

    g3 = s3_chunk_gen(0)

    def f0_before(tb):
        s1_upto(tiles_of(tb)[-1])
        if tb == 0:
            next(g3)
        if tb == 2:
            w_issue()

    def f0_after(tb):
        try:
            next(g3)
        except StopIteration:
            pass

    def s1_hook():
        if s1k[0] < NT + 2:
            s1_step()

    s1_upto(5)
    mm_hook[0] = s1_hook
    s2_front(0, before_tb=f0_before, after_tb=f0_after)
    mm_hook[0] = None
    for g_ in range(4):
        for tb_ in range(5):
            Rpo[g_][tb_].inherit(*Rxin1[4:])
    s2_front(1)
    s2_back_pool(0); s2_back_d(0)
    s2_back_pool(1); s2_back_d(1); w_issue()
    s3_chunk(1)
    s2_front(2)
    s2_front(3)
    em.dma("sp", npp[:, :], ustage[113:128, 0, :], Dust, reads=[Rust], final=True)
    for s in range(16):
        em.dma("sp", nps[s, 7:15, :], ustage[s * 8:(s + 1) * 8, 1, :], Dust, reads=[Rust], final=True)
    s2_gmm(0)
    s2_back_pool(2); s2_back_d(2)
    s2_gmm(1)
    s2_back_pool(3); s2_back_d(3); w_issue()
    s3_chunk(2)
    s3_chunk(3)
    s2_gmm(2)
    s2_gmm(3)
    em.dma("sp", nps[:, 0:7, :], spool[:, 8:15, :], Dout, final=True)
    em.dma("sp", ncp[:, :], vstage[126:128, 0, :], Dvst, reads=[Rvst], final=True)
    for s in range(16):
        em.dma("sp", ncs[s, :, :], vstage[s * 8 + 6:s * 8 + 8, 1, :], Dvst, reads=[Rvst], final=True)

    Rs4 = [Res() for _ in range(8)]
    for r_ in Rs4:
        r_.inherit(Rt[0], Rt[1])
    for m_ in range(8):
        for tb in range(5):
            Rm[m_][tb].inherit(*Rxin1, *Rxn1[0], *Rxn1[1], Rconst, Rust, Rvst)
    em.dma("pool", wo_sb[:, :, :], w_o.rearrange("(k p) n -> p k n", p=128), Dwo, writes=[Rwo])
    s4c = [0]

    def s4_chunk(m):
        w_issue()
        sl_gp = load_w(); sl_gc = load_w(); sl_pu = load_w(); sl_co = load_w()
        for tb in range(5):
            n = NB[tb]
            c0 = CB[tb]
            hres = [Rh[t] for t in tiles_of(tb)]
            b1 = nb(); mm_fm(b1, n, sl_gp, 8, lambda k, c0=c0, n=n: hnT[:, k, c0:c0 + n], hres)
            b2 = nb(); mm_fm(b2, n, sl_pu, 4, lambda k, c0=c0, n=n: pool_out[:, k, c0:c0 + n], [Rpo[k][tb] for k in range(4)])
            b3 = nb(); mm_fm(b3, n, sl_gc, 8, lambda k, c0=c0, n=n: hnT[:, k, c0:c0 + n], hres)
            b4 = nb(); mm_fm(b4, n, sl_co, 4, lambda k, c0=c0, n=n: conv_out[:, k, c0:c0 + n], [Rco[k][tb] for k in range(4)])
            q = (s4c[0] % 2) * 4
            s4c[0] += 1
            sgp, sgc, t1, t2 = s4buf[q], s4buf[q + 1], s4buf[q + 2], s4buf[q + 3]
            em.op("act", "activation", dict(out=sgp[:, 0:n], in_=psum[:, b1, 0:n], func=AF.Sigmoid),
                  reads=[Rps[b1]], writes=[Rs4[q]])
            em.op("dve", "tensor_tensor", dict(out=t1[:, 0:n], in0=psum[:, b2, 0:n], in1=sgp[:, 0:n], op=ALU.mult),
                  reads=[Rps[b2], Rs4[q]], writes=[Rs4[q + 2]])
            em.op("act", "activation", dict(out=sgc[:, 0:n], in_=psum[:, b3, 0:n], func=AF.Sigmoid),
                  reads=[Rps[b3]], writes=[Rs4[q + 1]])
            em.op("dve", "tensor_tensor", dict(out=t2[:, 0:n], in0=psum[:, b4, 0:n], in1=sgc[:, 0:n], op=ALU.mult),
                  reads=[Rps[b4], Rs4[q + 1]], writes=[Rs4[q + 3]])
            em.op("pool", "tensor_tensor", dict(out=merged[:, m, c0:c0 + n], in0=t1[:, 0:n], in1=t2[:, 0:n], op=ALU.add),
                  reads=[Rs4[q + 2], Rs4[q + 3]], writes=[Rm[m][tb]])
        w_done(sl_gp, sl_gc, sl_pu, sl_co)

    for m in range(8):
        s4_chunk(m)

    Rxin5 = [Res(), Res()]; Dxin5 = [em.newsem("dxin5a"), em.newsem("dxin5b")]
    Rxn5 = [Res(), Res(), Res()]
    allco = [Rco[k][tb] for k in range(4) for tb in range(5)]
    for r_ in Rxin5 + Rxn5:
        r_.inherit(*allco)
    alls4 = Rs4 + [Rd[0], Rd[1], Ru[0], Ru[1]] + [Rv[0][tb] for tb in range(5)] + [Rvpre[0]] + Rcsb + Ry[0:2] + [Rctm]
    for t in range(NT):
        Rx[t].inherit(*alls4)
    allpo = [Rpo[k][tb] for k in range(4) for tb in range(5)]

    def s5_a(t):
        i = t % 2
        tb = tb_of(t)
        em.dma("sp", xin5[i][:, :], xtile(t), Dxin5[i], writes=[Rxin5[i]])
        b0 = nb2()
        mm_tm(b0, lambda k, t=t: merged[:, k, t * 128:(t + 1) * 128], [Rm[k][tb] for k in range(8)],
              lambda k, half: wo_sb[:, k, half * 512:(half + 1) * 512], [Rwo], 8)
        em.op("dve", "tensor_tensor", dict(out=x_res[:, t, :], in0=psum[:, b0:b0 + 2, :].rearrange("p a b -> p (a b)"),
                                           in1=xin5[i][:, :], op=ALU.add),
              reads=[Rps[b0], Rps[b0 + 1], Rxin5[i]], writes=[Rx[t]])

    for k in range(NT + 3):
        if k < NT:
            s5_a(k)
        if 0 <= k - 1 < NT:
            norm_front(x_res[:, k - 1, :], Rx[k - 1], xn5[(k - 1) % 3], Rxn5[(k - 1) % 3])
        if 0 <= k - 3 < NT:
            norm_back(xn5[(k - 3) % 3], Rxn5[(k - 3) % 3], 1, k - 3)

    Ract = [[Res() for _ in range(5)] for _ in range(4)]
    for fl in range(4):
        for tb in range(5):
            Ract[fl][tb].inherit(*allpo)
    Rsg6 = [Res(), Res()]; Rxn6 = [Res(), Res(), Res()]
    for r_ in Rsg6 + Rxn6:
        r_.inherit(*allco, *Rxin5, *Rxn5)
    Rwfo = [Res(), Res()]; Dwfo = [em.newsem("dwfoa"), em.newsem("dwfob")]
    allm = [Rm[k][tb] for k in range(8) for tb in range(5)]
    for r_ in Rwfo:
        r_.inherit(*allm)
    Rwpg = Res().inherit(*allm); Dwpg = em.newsem("dwpg")
    Rwple = Res().inherit(Rwo); Dwple = em.newsem("dwple")
    Rgfin = Res().inherit(Rwo); Dgfin = em.newsem("dgfin")
    sgc6 = [0]

    def s6_in(part, f0, nf, fl):
        wi = part % 2
        f = f0 + fl
        w_issue()
        sl_g = load_w(); sl_u = load_w()
        if fl == 0:
            em.dma("pool", wfo[wi][:, 0:nf, :],
                   w_ffn_out[f0 * 128:(f0 + nf) * 128, :].rearrange("(f p) n -> p f n", p=128),
                   Dwfo[wi], writes=[Rwfo[wi]])
            if part == 5:
                for kh in range(2):
                    em.op("pool", "tensor_tensor", dict(
                        out=wpg_sb[:, 4 * kh:4 * kh + 4, :], in0=wpg_sb[:, 4 * kh:4 * kh + 4, :],
                        in1=gT[2][:, 4 * kh:4 * kh + 4].unsqueeze(2).to_broadcast([128, 4, D]), op=ALU.mult),
                        reads=[Rg0], writes=[Rwpg])
            if part == 4:
                em.dma("pool", wpg_sb[:, :, :], w_ple_gate.rearrange("(k p) n -> p k n", p=128), Dwpg, writes=[Rwpg])
                em.dma("pool", wple_sb[:, :, :], w_ple.rearrange("(k p) n -> p k n", p=128), Dwple, writes=[Rwple])
                em.dma("sp", gfin[:, :], g_final.partition_broadcast(128), Dgfin, writes=[Rgfin])
        for tb in range(5):
            n = NB[tb]
            c0 = CB[tb]
            hres = [Rh[t] for t in tiles_of(tb)]
            bg = nb(); mm_fm(bg, n, sl_g, 8, lambda k, c0=c0, n=n: hnT[:, k, c0:c0 + n], hres)
            bu = nb(); mm_fm(bu, n, sl_u, 8, lambda k, c0=c0, n=n: hnT[:, k, c0:c0 + n], hres)
            si = sgc6[0] % 2
            sgc6[0] += 1
            em.op("act", "activation", dict(out=sg6[si][:, 0:n], in_=psum[:, bg, 0:n], func=AF.Silu),
                  reads=[Rps[bg]], writes=[Rsg6[si]])
            em.op("dve", "tensor_tensor", dict(out=act_sb[:, fl, c0:c0 + n], in0=psum[:, bu, 0:n], in1=sg6[si][:, 0:n], op=ALU.mult),
                  reads=[Rps[bu], Rsg6[si]], writes=[Ract[fl][tb]])
        w_done(sl_g, sl_u)

    def s6_out(part, nf, t):
        wi = part % 2
        tb = tb_of(t)
        b0 = nb2()
        mm_tm(b0, lambda k, t=t: act_sb[:, k, t * 128:(t + 1) * 128], [Ract[k][tb] for k in range(nf)],
              lambda k, half, wi=wi: wfo[wi][:, k, half * 512:(half + 1) * 512], [Rwfo[wi]], nf)
        em.op("dve", "tensor_tensor", dict(out=x_res[:, t, :], in0=psum[:, b0:b0 + 2, :].rearrange("p a b -> p (a b)"),
                                           in1=x_res[:, t, :], op=ALU.add),
              reads=[Rps[b0], Rps[b0 + 1]], writes=[Rx[t]])

    f0 = 0
    for part, nf in enumerate(NPART):
        for fl in range(nf):
            s6_in(part, f0, nf, fl)
        if part < len(NPART) - 1:
            for t in range(NT):
                s6_out(part, nf, t)
        else:
            def s6_norm_steps(k):
                if 0 <= k - 1 < NT:
                    norm_front(x_res[:, k - 1, :], Rx[k - 1], xn6[(k - 1) % 3], Rxn6[(k - 1) % 3], scale_eng="pool")
                if 0 <= k - 3 < NT:
                    norm_back(xn6[(k - 3) % 3], Rxn6[(k - 3) % 3], 2, k - 3)

            for k in range(NT):
                s6_out(part, nf, k)
                s6_norm_steps(k)
        f0 += nf

    allact = [Ract[fl][tb] for fl in range(4) for tb in range(5)]
    mk7 = lambda n=2: [Res().inherit(*allact, *Rsg6, *Rxn6) for _ in range(n)]
    mkwo = lambda n: [Res().inherit(Rwo, Rwple, Rgfin) for _ in range(n)]
    Rsg7 = mk7(); Rtt7 = mk7(); Rpin = mkwo(4); Rpbf = mkwo(3); RpT = mkwo(2)
    Dpin = [em.newsem("dpin%d" % i) for i in range(4)]
    Dyo = [em.newsem("dyo%d" % i) for i in range(4)]
    s7b = {}
    c7 = statc[0]
    statc[0] += 2 * NT
    GRP = 2

    def s7_pin(t):
        if t < NT:
            em.dma("sp", pin[t % 4][:, :], ptile(t), Dpin[t % 4], writes=[Rpin[t % 4]])

    def s7_cast(t):
        if t < NT:
            em.op("pool", "tensor_tensor", dict(out=pbf[t % 3][:, :], in0=pin[t % 4][:, :], in1=yst[0][:, 0:256], op=ALU.add),
                  reads=[Rpin[t % 4], Rz7], writes=[Rpbf[t % 3]])

    def s7_a(t):
        i = t % 2
        s7_pin(t + 3)
        s7_cast(t + 1)
        b = nb()
        for c in range(2):
            em.op("pe", "transpose", dict(out=psum_bf[:, b, c * 128:(c + 1) * 128],
                                          in_=pbf[t % 3][:, c * 128:(c + 1) * 128], identity=ident[:, :]),
                  reads=[Rpbf[t % 3], Rident], writes=[Rps[b]], sig=(c == 1))
        em.op("dve", "tensor_copy", dict(out=pT[i][:, :, :], in_=psum_bf[:, b, 0:256].rearrange("p (c t) -> p c t", c=2)),
              reads=[Rps[b]], writes=[RpT[i]])

    def s7_mm(t):
        i = t % 2
        bg = nb2()
        mm_tm(bg, lambda k, t=t: hnT[:, k, t * 128:(t + 1) * 128], [Rh[t]],
              lambda k, half: wpg_sb[:, k, half * 512:(half + 1) * 512], [Rwpg], 8)
        bp = nb2()
        mm_tm(bp, lambda k, i=i: pT[i][:, k, :], [RpT[i]],
              lambda k, half: wple_sb[:, k, half * 512:(half + 1) * 512], [Rwple], 2)
        s7b[t] = (bg, bp)

    def s7_b(t):
        i = t % 2
        bg, bp = s7b[t]
        em.op("act", "activation", dict(out=sg7[i][:, :], in_=psum[:, bg:bg + 2, :].rearrange("p a b -> p (a b)"), func=AF.Sigmoid),
              reads=[Rps[bg], Rps[bg + 1]], writes=[Rsg7[i]])
        em.op("dve", "tensor_tensor", dict(out=tt7[i][:, :], in0=psum[:, bp:bp + 2, :].rearrange("p a b -> p (a b)"),
                                           in1=sg7[i][:, :], op=ALU.mult),
              reads=[Rps[bp], Rps[bp + 1], Rsg7[i]], writes=[Rtt7[i]])
        em.op("pool", "tensor_tensor", dict(out=x_res[:, t, :], in0=x_res[:, t, :], in1=tt7[i][:, :], op=ALU.add),
              reads=[Rtt7[i]], writes=[Rx[t]])

    Rms7 = {}

    def s7_c(t):
        i = t % 2
        g0 = (t // GRP) * GRP
        g1 = min(g0 + GRP, NT)
        if t == g0:
            Rms7[g0] = Res()
        Rms = Rms7[g0]
        em.op("act", "activation", dict(out=tt7[i][:, :], in_=x_res[:, t, :], func=AF.Square, scale=1.0 / 32.0,
                                        accum_out=stat[:, c7 + t:c7 + t + 1]),
              reads=[Rx[t]], writes=[Rtt7[i], Rms])
        if t == g1 - 1:
            ms = stat[:, c7 + g0:c7 + g1]
            rs = stat[:, c7 + NT + g0:c7 + NT + g1]
            Rrs = Res()
            em.op("act", "activation", dict(out=ms, in_=ms, func=AF.Sqrt, bias=EPS, scale=1.0), reads=[Rms], writes=[Rms])
            em.op("dve", "reciprocal", dict(out=rs, in_=ms), reads=[Rms], writes=[Rrs])
            for tt in range(g0, g1):
                em.op("dve", "scalar_tensor_tensor", dict(out=x_res[:, tt, :], in0=x_res[:, tt, :],
                                                          scalar=stat[:, c7 + NT + tt:c7 + NT + tt + 1], in1=gfin[:, :],
                                                          op0=ALU.mult, op1=ALU.mult),
                      reads=[Rrs, Rgfin], writes=[Rx[tt]])
                em.dma("sp", ytile(tt), x_res[:, tt, :], Dyo[tt % 4], reads=[Rx[tt]], final=True)

    Rz7 = mk7(1)[0]
    em.op("pool", "memset", dict(ap=yst[0][:, 0:256], constant=0.0), writes=[Rz7])
    s7_pin(0)
    s7_pin(1)
    s7_pin(2)
    s7_cast(0)
    for k in range(NT + 2):
        if k < 3:
            s6_norm_steps(NT + k)
        if k < NT:
            s7_a(k)
        if 0 <= k - 1 < NT:
            s7_b(k - 1)
        if k < NT:
            s7_mm(k)
        if 0 <= k - 2 < NT:
            s7_c(k - 2)

    assert wst["taken"] == len(WL) and wst["issued"] == len(WL), wst
    em.finish()
    em.run()
    return nc


_NC_CACHE = {}


def kernel(x_prompt, x_sample, state_pool, state_conv, p_prompt, p_sample,
           g_mix, w_in, w_pool_group, pool_scale, w_pool_up, w_conv, w_conv_out, w_o,
           g_ffn, w_ffn_in, w_ffn_out, g_ple, w_ple, w_ple_gate, g_final):
    f = lambda a: np.ascontiguousarray(np.asarray(a, dtype=np.float32))
    x_prompt, x_sample, state_pool, state_conv, p_prompt, p_sample = map(
        f, (x_prompt, x_sample, state_pool, state_conv, p_prompt, p_sample))
    shared = dict(
        g_mix=f(g_mix)[0], w_in=f(w_in)[0], w_pg=f(w_pool_group)[0], pool_scale=f(pool_scale)[0],
        w_pool_up=f(w_pool_up)[0], w_conv=f(w_conv)[0], w_conv_out=f(w_conv_out)[0], w_o=f(w_o)[0],
        g_ffn=f(g_ffn)[0], w_ffn_in=f(w_ffn_in)[0], w_ffn_out=f(w_ffn_out)[0], g_ple=f(g_ple)[0],
        w_ple=f(w_ple)[0], w_ple_gate=f(w_ple_gate)[0], g_final=f(g_final))
    nc = build_nc()
    in_maps = []
    for c in range(8):
        m = dict(shared)
        m["xp"] = x_prompt[c]
        m["xs"] = x_sample[16 * c:16 * (c + 1)].reshape(128, D)
        m["spool"] = state_pool[0, 16 * c:16 * (c + 1)]
        m["sconv"] = state_conv[0, 16 * c:16 * (c + 1)]
        m["ppr"] = p_prompt[0, c]
        m["psa"] = p_sample[0, 16 * c:16 * (c + 1)].reshape(128, 256)
        in_maps.append(m)
    res = run_bass_kernel_spmd(nc, in_maps, core_ids=list(range(8)))
    R = res.results
    y_prompt = np.stack([np.asarray(R[c]["yp"], dtype=np.float32) for c in range(8)], axis=0)
    y_sample = np.concatenate([np.asarray(R[c]["ys"], dtype=np.float32).reshape(16, 8, D) for c in range(8)], axis=0)
    npp = np.stack([np.asarray(R[c]["npp"], dtype=np.float32) for c in range(8)], axis=0)[None]
    ncp = np.stack([np.asarray(R[c]["ncp"], dtype=np.float32) for c in range(8)], axis=0)[None]
    nps = np.concatenate([np.asarray(R[c]["nps"], dtype=np.float32) for c in range(8)], axis=0)[None]
    ncs = np.concatenate([np.asarray(R[c]["ncs"], dtype=np.float32) for c in range(8)], axis=0)[None]
    return (y_prompt, y_sample, npp, ncp, nps, ncs)
```

```python
import numpy as np
import concourse.bass as bass
import concourse.mybir as mybir
from concourse.bass_utils import run_bass_kernel_spmd

F32 = mybir.dt.float32
BF16 = mybir.dt.bfloat16
ALU = mybir.AluOpType
AF = mybir.ActivationFunctionType

D = 1024
SEQ = 2048
NT = 17
NTOK = NT * 128
NB = [512, 512, 512, 512, 128]
CB = [0, 512, 1024, 1536, 2048]
DFF = 2816
NF = 22
EPS = 1e-6
SB_BASE = 16640
SB_END = 229376


class Sem:
    def __init__(self, nc, name):
        self.h = nc.alloc_semaphore(name)
        self.cnt = 0


class Res:
    __slots__ = ("name", "w", "r")

    def __init__(self, name="r"):
        self.name = name
        self.w = []
        self.r = []

    def inherit(self, *others):
        for o in others:
            self.w = self.w + o.w + o.r
        return self


class Em:
    def __init__(self, nc):
        self.nc = nc
        self.engs = {}
        for n in ["pe", "act", "dve", "pool", "sp"]:
            self.engs[n] = dict(sem=Sem(nc, "s_" + n), seen={}, thunks=[])
        self.final = []
        self.nsem = 5

    def newsem(self, name):
        self.nsem += 1
        return Sem(self.nc, name)

    def _waits(self, e, reads, writes):
        E = self.engs[e]
        need = {}
        for r in reads:
            for (s, v) in r.w:
                if need.get(s, 0) < v:
                    need[s] = v
        for w in writes:
            for (s, v) in w.w:
                if need.get(s, 0) < v:
                    need[s] = v
            for (s, v) in w.r:
                if need.get(s, 0) < v:
                    need[s] = v
        out = []
        for s, v in need.items():
            if s is E["sem"]:
                if e == "pe":
                    continue
                if v > s.cnt:
                    continue
            if E["seen"].get(s, 0) < v:
                E["seen"][s] = v
                out.append((s, v))
        return out

    @staticmethod
    def _mark(tok, reads, writes):
        for r in reads:
            r.r = [t for t in r.r if t[0] is not tok[0]] + [tok]
        for w in writes:
            w.w = [tok]
            w.r = []

    def op(self, e, meth, kw, reads=(), writes=(), sig=True):
        E = self.engs[e]
        fn = (lambda eng, meth=meth, kw=kw: getattr(eng, meth)(**kw))
        waits = self._waits(e, reads, writes)
        s = E["sem"]
        if sig:
            s.cnt += 1
            tok = (s, s.cnt)
        else:
            tok = (s, s.cnt + 1)

        def thunk(eng, waits=waits, fn=fn, sig=sig, s=s):
            for (ws, wv) in waits:
                eng.wait_ge(ws.h, wv)
            ins = fn(eng)
            if sig:
                ins.then_inc(s.h, 1)
        E["thunks"].append(thunk)
        self._mark(tok, reads, writes)
        return tok

    def dma(self, q, out, in_, dsem, reads=(), writes=(), final=False, **kw):
        E = self.engs[q]
        waits = self._waits(q, reads, writes)
        dsem.cnt += 16
        tok = (dsem, dsem.cnt)

        def thunk(eng, waits=waits, out=out, in_=in_, kw=kw, dsem=dsem):
            for (ws, wv) in waits:
                eng.wait_ge(ws.h, wv)
            eng.dma_start(out=out, in_=in_, **kw).then_inc(dsem.h, 16)
        E["thunks"].append(thunk)
        self._mark(tok, reads, writes)
        if final:
            self.final.append(tok)
        return tok

    def finish(self):
        need = {}
        for (s, v) in self.final:
            need[s] = max(need.get(s, 0), v)
        lst = list(need.items())

        def thunk(eng):
            for (s, v) in lst:
                eng.wait_ge(s.h, v)
        self.engs["sp"]["thunks"].append(thunk)

    def run(self):
        nc = self.nc
        for n, E in self.engs.items():
            assert E["sem"].cnt < 60000, (n, E["sem"].cnt)
        with nc.Block() as block:
            @block.tensor
            def _(eng):
                for t in self.engs["pe"]["thunks"]:
                    t(eng)

            @block.scalar
            def _(eng):
                for t in self.engs["act"]["thunks"]:
                    t(eng)

            @block.vector
            def _(eng):
                for t in self.engs["dve"]["thunks"]:
                    t(eng)

            @block.gpsimd
            def _(eng):
                for t in self.engs["pool"]["thunks"]:
                    t(eng)

            @block.sync
            def _(eng):
                for t in self.engs["sp"]["thunks"]:
                    t(eng)


def build_nc():
    nc = bass.Bass("TRN2", target_bir_lowering=False)

    def din(name, shape):
        return nc.dram_tensor(name, list(shape), F32, kind="ExternalInput").ap()

    def dout(name, shape):
        return nc.dram_tensor(name, list(shape), F32, kind="ExternalOutput").ap()

    xp = din("xp", [SEQ, D]); xs = din("xs", [128, D])
    spool = din("spool", [16, 15, 512]); sconv = din("sconv", [16, 2, 512])
    ppr = din("ppr", [SEQ, 256]); psa = din("psa", [128, 256])
    g_mix = din("g_mix", [D]); w_in = din("w_in", [D, 4096])
    w_pg = din("w_pg", [4, 128, 128]); pool_scale = din("pool_scale", [512])
    w_pool_up = din("w_pool_up", [512, D]); w_conv = din("w_conv", [3, 512])
    w_conv_out = din("w_conv_out", [512, D]); w_o = din("w_o", [D, D])
    g_ffn = din("g_ffn", [D]); w_ffn_in = din("w_ffn_in", [D, 2 * DFF])
    w_ffn_out = din("w_ffn_out", [DFF, D]); g_ple = din("g_ple", [D])
    w_ple = din("w_ple", [256, D]); w_ple_gate = din("w_ple_gate", [D, D])
    g_final = din("g_final", [D])

    yp = dout("yp", [SEQ, D]); ys = dout("ys", [128, D])
    npp = dout("npp", [15, 512]); ncp = dout("ncp", [2, 512])
    nps = dout("nps", [16, 15, 512]); ncs = dout("ncs", [16, 2, 512])

    em = Em(nc)

    def xtile(t):
        return xp[t * 128:(t + 1) * 128, :] if t < 16 else xs[:, :]

    def ytile(t):
        return yp[t * 128:(t + 1) * 128, :] if t < 16 else ys[:, :]

    def ptile(t):
        return ppr[t * 128:(t + 1) * 128, :] if t < 16 else psa[:, :]

    cnt = [0]

    def sbt(off, shape, dt):
        nbytes = int(np.prod(shape[1:])) * (4 if dt == F32 else 2)
        assert off % 32 == 0, off
        assert SB_BASE <= off and off + nbytes <= SB_END, (off, nbytes)
        cnt[0] += 1
        return nc.alloc_sbuf_tensor_at("t%d" % cnt[0], list(shape), dt, offset=off), off + nbytes

    A0 = SB_BASE
    B0 = A0 + NT * 4096
    C0 = B0 + 8 * NTOK * 2
    D0 = C0 + 69632
    assert D0 + 38656 <= SB_END

    x_res, _ = sbt(A0, [128, NT, D], F32)
    hnT, _ = sbt(B0, [128, 8, NTOK], BF16)
    pool_out, _ = sbt(C0, [128, 4, NTOK], BF16)
    conv_out, _ = sbt(C0 + 17408, [128, 4, NTOK], BF16)
    merged, _ = sbt(C0 + 34816, [128, 8, NTOK], BF16)
    NSLOT = 8
    wslot = []
    o = D0
    for i in range(NSLOT):
        t_, o = sbt(o, [128, 8, 128], BF16)
        wslot.append(t_)
    wo_sb, o = sbt(o, [128, 8, D], BF16)
    WO_OFF = o - 16384
    ident, o = sbt(o, [128, 128], BF16)
    identf, o = sbt(o, [128, 128], F32)
    gTall, o = sbt(o, [128, 32], F32)
    gT = [gTall[:, 8 * i:8 * i + 8] for i in range(3)]
    pscale = gTall[:, 24:32]
    cst1, o = sbt(o, [128, 128], F32)
    cst2, o = sbt(o, [128, 128], F32)
    wconv, o = sbt(o, [128, 4, 4], F32)
    wg_sb, o = sbt(o, [128, 4, 128], BF16)
    rcnt, o = sbt(o, [128, 4, 16], F32)
    stat, o = sbt(o, [128, 4 * NT * 2], F32)
    hdbuf, o = sbt(o, [128, 16], F32)
    assert o <= SB_END, o

    LU = 15 + SEQ + 16 * 23
    UB = 9728
    o = A0
    ubuf = []
    for i in range(2):
        t_, o2 = sbt(o, [128, LU], F32)
        ubuf.append(t_); o += UB
    tbuf = []
    for i in range(2):
        t_, o2 = sbt(o, [128, LU], F32)
        tbuf.append(t_); o += UB
    dbuf = []
    for i in range(2):
        t_, o = sbt(o, [128, NTOK], BF16)
        dbuf.append(t_)
    LV = 2 + SEQ + 16 * 10
    vb_, o = sbt(o, [128, LV], F32)
    o = (o + 31) // 32 * 32
    vbuf = [vb_, vb_]
    csb = []
    for i in range(2):
        t_, o = sbt(o, [128, 512], F32)
        csb.append(t_)
    ybuf = []
    for i in range(2):
        t_, o = sbt(o, [128, 512], F32)
        ybuf.append(t_)
    ybuf = ybuf + ybuf
    ctm, o = sbt(o, [128, 2, 128], F32)
    assert o <= B0, o
    o = A0 + 2 * UB
    s4buf = []
    for i in range(8):
        t_, o = sbt(o, [128, 512], F32)
        s4buf.append(t_)
    assert o <= A0 + 4 * UB
    o = C0 + 34816
    NX1 = 8
    xin1 = []
    for i in range(4):
        t_, o = sbt(o, [128, D], F32)
        xin1.append(t_)
    for i in range(4):
        t_, _ = sbt(C0 + i * 4096, [128, D], F32)
        xin1.append(t_)
    xn1 = []
    for i in range(2):
        t_, o = sbt(o, [128, D], BF16)
        xn1.append(t_)
    stp = []
    for i in range(2):
        t_, o = sbt(o, [128, 512], F32)
        stp.append(t_)
    stc, o = sbt(o, [128, 512], F32)
    ustage, o = sbt(o, [128, 2, 512], F32)
    vstage, o = sbt(o, [128, 2, 512], F32)
    assert o <= C0 + 69632, o
    o = C0 + 17408
    xin5 = []
    for i in range(2):
        t_, o = sbt(o, [128, D], F32)
        xin5.append(t_)
    xn5 = []
    for i in range(3):
        t_, o = sbt(o, [128, D], BF16)
        xn5.append(t_)
    assert o <= C0 + 34816
    NPART = [3, 3, 4, 4, 4, 4]
    act_sb, _ = sbt(C0, [128, 4, NTOK], BF16)
    o = C0 + 17408
    sg6 = []
    for i in range(2):
        t_, o = sbt(o, [128, 512], F32)
        sg6.append(t_)
    xn6 = []
    for i in range(3):
        t_, o = sbt(o, [128, D], BF16)
        xn6.append(t_)
    assert o <= C0 + 34816
    o = C0 + 34816
    wfo = []
    for i in range(2):
        t_, o = sbt(o, [128, 4, D], BF16)
        wfo.append(t_)
    wpg_sb, o = sbt(o, [128, 8, D], BF16)
    assert o <= D0
    o = C0
    sg7 = []
    for i in range(2):
        t_, o = sbt(o, [128, D], F32)
        sg7.append(t_)
    tt7 = []
    for i in range(2):
        t_, o = sbt(o, [128, D], F32)
        tt7.append(t_)
    yst = []
    for i in range(2):
        t_, o = sbt(o, [128, D], F32)
        yst.append(t_)
    assert o <= C0 + 34816
    wple_sb, o = sbt(WO_OFF, [128, 2, D], BF16)
    gfin, o = sbt(o, [128, D], F32)
    pin = []
    for i in range(4):
        t_, o = sbt(o, [128, 256], F32)
        pin.append(t_)
    pbf = []
    for i in range(3):
        t_, o = sbt(o, [128, 256], BF16)
        pbf.append(t_)
    pT = []
    for i in range(2):
        t_, o = sbt(o, [128, 2, 128], BF16)
        pT.append(t_)
    assert o <= WO_OFF + 16384

    psum = nc.alloc_psum_tensor("psum", [128, 8, 512], F32)
    psum_bf = psum.bitcast(BF16)

    Rps = [Res("ps%d" % i) for i in range(8)]
    Rh = [Res() for _ in range(NT)]
    Rx = [Res() for _ in range(NT)]
    Rpo = [[Res() for _ in range(5)] for _ in range(4)]
    Rco = [[Res() for _ in range(5)] for _ in range(4)]
    Rm = [[Res() for _ in range(5)] for _ in range(8)]
    Rw = [Res() for _ in range(NSLOT)]
    Dw = [em.newsem("dw%d" % i) for i in range(NSLOT)]
    Rwo = Res(); Dwo = em.newsem("dwo")
    Rconst = Res(); Dconst = em.newsem("dconst")
    Rident = Res(); Ridentf = Res(); Rrcnt = Res()
    Ru = [Res(), Res()]; Rt = [Res(), Res()]; Rd = [Res(), Res()]
    Rust = Res(); Rvst = Res(); Dust = em.newsem("dust"); Dvst = em.newsem("dvst")
    Dout = em.newsem("dout")
    Dmisc = em.newsem("dmisc")

    bankc = [0]

    def nb():
        b = bankc[0] % 8
        bankc[0] += 1
        return b

    def nb2():
        if bankc[0] % 2:
            bankc[0] += 1
        b = bankc[0] % 8
        bankc[0] += 2
        return b

    wcol = lambda src, c0: src[:, c0:c0 + 128].rearrange("(k p) n -> p k n", p=128)
    WL = []
    def wl_s3(cc):
        WL.append((wcol(w_in, 512 + cc * 128), 8))
        WL.append((wcol(w_in, 1024 + cc * 128), 8))
        WL.append((wcol(w_in, 1536 + cc * 128), 8))
    wl_u = lambda g: WL.append((wcol(w_in, g * 128), 8))
    wl_u(0); wl_s3(0); wl_u(1); wl_s3(1); wl_u(2); wl_u(3); wl_s3(2); wl_s3(3)
    for m_ in range(8):
        WL.append((wcol(w_in, 2048 + m_ * 128), 8))
        WL.append((wcol(w_in, 3072 + m_ * 128), 8))
        WL.append((wcol(w_pool_up, m_ * 128), 4))
        WL.append((wcol(w_conv_out, m_ * 128), 4))
    for f_ in range(NF):
        WL.append((wcol(w_ffn_in, f_ * 128), 8))
        WL.append((wcol(w_ffn_in, DFF + f_ * 128), 8))
    wst = dict(issued=0, taken=0, done=0)

    def w_issue(limit=None):
        while wst["issued"] < len(WL) and wst["issued"] < wst["done"] + NSLOT and (limit is None or wst["issued"] < limit):
            j = wst["issued"]
            src_ap, kn = WL[j]
            em.dma("pool", wslot[j % NSLOT][:, 0:kn, :], src_ap, Dw[j % NSLOT], writes=[Rw[j % NSLOT]])
            wst["issued"] += 1

    slot_idx = {}
    wflags = [False] * len(WL)

    def load_w():
        j = wst["taken"]
        wst["taken"] += 1
        assert j < wst["issued"], (j, wst)
        slot_idx[j % NSLOT] = j
        return j % NSLOT

    def w_done(*slots):
        for sl in slots:
            wflags[slot_idx[sl]] = True
        while wst["done"] < len(WL) and wflags[wst["done"]]:
            wst["done"] += 1

    def tiles_of(tb):
        return list(range(4 * tb, 4 * tb + 4)) if tb < 4 else [16]

    def tb_of(t):
        return t // 4 if t < 16 else 4

    def cdma(out, in_, **kw):
        em.dma("act", out, in_, Dconst, writes=[Rconst], **kw)
        Rconst.w = []

    Rg0 = Res(); Rwc = Res(); Dc1 = em.newsem("dc1")
    Rc1m = Res()
    em.op("pool", "memset", dict(ap=cst1[0:32, :], constant=0.0), writes=[Rc1m])
    Rrow = []
    for r0, src_ap, nrow in [(0, g_mix, 8), (8, g_ffn, 8), (16, g_ple, 8), (24, pool_scale, 4)]:
        rr = Res().inherit(Rc1m)
        em.dma("act", cst1[r0:r0 + nrow, :], src_ap.rearrange("(c p) -> c p", p=128), Dc1, writes=[rr])
        Rrow.append(rr)
    rr = Res()
    em.dma("act", cst2[0:12, :], w_conv.rearrange("k (c p) -> (k c) p", p=128), Dc1, writes=[rr])
    Rrow.append(rr)
    cdma(stp[0][0:120, :], spool[0:8, :, :].rearrange("s r c -> (s r) c"))
    cdma(stp[1][0:120, :], spool[8:16, :, :].rearrange("s r c -> (s r) c"))
    cdma(stc[0:32, :], sconv.rearrange("s r c -> (s r) c"))
    Rconst.w = [(Dconst, Dconst.cnt)]
    Rwg = Res(); Dwg = em.newsem("dwg")
    em.dma("pool", wg_sb[:, :, :], w_pg.rearrange("g c d -> c g d"), Dwg, writes=[Rwg])

    def mk_ident(t, r):
        em.op("pool", "memset", dict(ap=t[:, :], constant=1.0), writes=[r])
        em.op("pool", "affine_select", dict(
            out=t[:, :], in_=t[:, :], pattern=[[-1, 128]], compare_op=ALU.is_equal,
            fill=0.0, base=0, channel_multiplier=1), reads=[r], writes=[r])
    mk_ident(ident, Rident)
    mk_ident(identf, Ridentf)
    bq = 7
    em.op("pe", "transpose", dict(out=psum[:, bq, 0:32], in_=cst1[0:32, :], identity=identf[0:32, 0:32]),
          reads=Rrow + [Ridentf], writes=[Rps[bq]], sig=False)
    em.op("pe", "transpose", dict(out=psum[:, bq, 32:44], in_=cst2[0:12, :], identity=identf[0:12, 0:12]),
          reads=[Rrow[4], Ridentf], writes=[Rps[bq]])
    em.op("dve", "tensor_copy", dict(out=gTall[:, :], in_=psum[:, bq, 0:32]), reads=[Rps[bq]], writes=[Rg0])
    em.op("dve", "tensor_copy", dict(out=wconv[:, :, 0:3], in_=psum[:, bq, 32:44].rearrange("p (k c) -> p c k", k=3)),
          reads=[Rps[bq]], writes=[Rwc])
    NWARM = 72
    for i in range(NWARM):
        em.op("pe", "matmul", dict(out=psum[:, 6, 0:128], lhsT=ident[:, :], rhs=ident[:, :], start=True, stop=True),
              reads=[Rident], writes=[Rps[6]], sig=(i == NWARM - 1))
    wins = [2, 4, 8, 16]
    for g in range(4):
        em.op("pool", "memset", dict(ap=rcnt[:, g, :], constant=1.0 / wins[g]), writes=[Rrcnt])
        for t in range(wins[g] - 1):
            em.op("pool", "memset", dict(ap=rcnt[:, g, t:t + 1], constant=1.0 / (t + 1)), writes=[Rrcnt])
    for i in range(2):
        em.op("pool", "memset", dict(ap=ubuf[i][:, 0:15], constant=0.0), writes=[Ru[i]])

    statc = [0]

    def norm_front(src_ap, src_res, xn_t, xn_res, scale_eng="dve"):
        c = statc[0]
        statc[0] += 2
        ms = stat[:, c:c + 1]
        rs = stat[:, c + 1:c + 2]
        Rms = Res(); Rrs = Res()
        if not isinstance(xn_res, list):
            xn_res = [xn_res]
        em.op("act", "activation", dict(out=xn_t[:, :], in_=src_ap, func=AF.Square, scale=1.0 / 32.0, accum_out=ms),
              reads=[src_res], writes=xn_res + [Rms])
        em.op("act", "activation", dict(out=ms, in_=ms, func=AF.Sqrt, bias=EPS, scale=1.0), reads=[Rms], writes=[Rms])
        em.op("dve", "reciprocal", dict(out=rs, in_=ms), reads=[Rms], writes=[Rrs])
        if scale_eng == "dve":
            em.op("dve", "tensor_scalar", dict(out=xn_t[:, :], in0=src_ap, scalar1=rs, scalar2=None, op0=ALU.mult),
                  reads=[src_res, Rrs], writes=xn_res)
        elif scale_eng == "pool":
            em.op("pool", "tensor_tensor", dict(out=xn_t[:, :], in0=src_ap, in1=rs.to_broadcast([128, D]), op=ALU.mult),
                  reads=[src_res, Rrs], writes=xn_res)
        elif scale_eng == "act":
            em.op("act", "activation", dict(out=xn_t[:, :], in_=src_ap, func=AF.Copy, scale=rs),
                  reads=[src_res, Rrs], writes=xn_res)
        else:
            em.op("pool", "tensor_tensor", dict(out=xn_t[:, 0:512], in0=src_ap[:, 0:512], in1=rs.to_broadcast([128, 512]), op=ALU.mult),
                  reads=[src_res, Rrs], writes=xn_res[0:1])
            em.op("dve", "tensor_scalar", dict(out=xn_t[:, 512:1024], in0=src_ap[:, 512:1024], scalar1=rs, scalar2=None, op0=ALU.mult),
                  reads=[src_res, Rrs], writes=xn_res[1:2])

    def norm_back(xn_t, xn_res, gidx, t):
        if not isinstance(xn_res, list):
            xn_res = [xn_res]
        b = nb()
        for cch in range(8):
            em.op("pe", "transpose", dict(out=psum_bf[:, b, cch * 128:(cch + 1) * 128],
                                          in_=xn_t[:, cch * 128:(cch + 1) * 128], identity=ident[:, :]),
                  reads=xn_res + [Rident], writes=[Rps[b]], sig=(cch == 7))
        if gidx == 2:
            em.op("dve", "tensor_copy", dict(out=hnT[:, :, t * 128:(t + 1) * 128],
                                             in_=psum_bf[:, b, :].rearrange("p (c t) -> p c t", c=8)),
                  reads=[Rps[b]], writes=[Rh[t]])
            return
        em.op("dve", "tensor_tensor", dict(
            out=hnT[:, :, t * 128:(t + 1) * 128],
            in0=psum_bf[:, b, :].rearrange("p (c t) -> p c t", c=8),
            in1=gT[gidx][:, :].unsqueeze(2).to_broadcast([128, 8, 128]), op=ALU.mult),
            reads=[Rps[b], Rg0], writes=[Rh[t]])

    def pipeline3(n, fa, fb, fc):
        for k in range(n + 2):
            if k < n:
                fa(k)
            if 0 <= k - 1 < n:
                fb(k - 1)
            if 0 <= k - 2 < n:
                fc(k - 2)

    def mm_fm(b, n, slot, kn, rhs_fn, rhs_res):
        for k in range(kn):
            em.op("pe", "matmul", dict(out=psum[:, b, 0:n], lhsT=wslot[slot][:, k, :], rhs=rhs_fn(k),
                                       start=(k == 0), stop=(k == kn - 1)),
                  reads=[Rw[slot]] + rhs_res, writes=[Rps[b]], sig=(k == kn - 1))
        if mm_hook[0] is not None:
            mm_hook[0]()

    mm_hook = [None]

    def mm_tm(b0, lhs_fn, lhs_res, rhs_fn, rhs_res, kn):
        for half in range(2):
            for k in range(kn):
                em.op("pe", "matmul", dict(out=psum[:, b0 + half, :], lhsT=lhs_fn(k), rhs=rhs_fn(k, half),
                                           start=(k == 0), stop=(k == kn - 1)),
                      reads=lhs_res + rhs_res, writes=[Rps[b0 + half]], sig=(k == kn - 1))

    w_issue(limit=4)

    Rv = [[Res() for _ in range(5)] for _ in range(2)]
    Rvpre = [Res(), Res()]
    Rcsb = [Res(), Res()]
    Ry = [Res() for _ in range(4)]
    Rctm = Res()
    Rv[1] = Rv[0]
    Rvpre[1] = Rvpre[0]
    Ry[2] = Ry[0]; Ry[3] = Ry[1]
    em.op("pool", "memset", dict(ap=vbuf[0][:, 0:2], constant=0.0), writes=[Rvpre[0]])

    Rxin1 = [Res() for _ in range(NX1)]; Dxin1 = [em.newsem("dxin1%d" % i) for i in range(NX1)]
    Rxn1 = [[Res(), Res()], [Res(), Res()]]
    s1k = [0]

    def s1_step():
        k = s1k[0]
        s1k[0] += 1
        for kk in (list(range(NX1 - 1)) if k == 0 else [k + NX1 - 2]):
            if kk < NT:
                em.dma("sp", xin1[kk % NX1][:, :], xtile(kk), Dxin1[kk % NX1],
                       reads=([Rxin1[(kk - 2) % NX1]] if kk >= 2 else ([Rxin1[0]] if kk == 1 else [])), writes=[Rxin1[kk % NX1]])
        if 0 <= k - 1 < NT:
            i = (k - 1) % 2
            ix = (k - 1) % NX1
            norm_front(xin1[ix][:, :], Rxin1[ix], xn1[i], Rxn1[i], scale_eng="split")
        if 0 <= k - 2 < NT:
            i = (k - 2) % 2
            norm_back(xn1[i], Rxn1[i], 0, k - 2)

    def s1_upto(t):
        while s1k[0] - 3 < t:
            s1_step()

    def us_view(buf):
        return buf[:, 15 + SEQ:LU].rearrange("p (s r) -> p s r", r=23)

    def vs_view(buf):
        return buf[:, 2 + SEQ:LV].rearrange("p (s r) -> p s r", r=10)

    def s2_front(g, before_tb=None, after_tb=None):
        ui = g % 2
        U = ubuf[ui]
        if g > 0:
            w_issue()
        slot = load_w()
        for tb in range(5):
            if before_tb is not None:
                before_tb(tb)
            n = NB[tb]
            b = nb()
            mm_fm(b, n, slot, 8, lambda k, tb=tb, n=n: hnT[:, k, CB[tb]:CB[tb] + n], [Rh[t] for t in tiles_of(tb)])
            if tb < 4:
                em.op("act", "activation", dict(out=U[:, 15 + CB[tb]:15 + CB[tb] + 512], in_=psum[:, b, :], func=AF.Copy),
                      reads=[Rps[b]], writes=[Ru[ui]])
            else:
                em.op("act", "activation", dict(out=us_view(U)[:, :, 15:23],
                                                in_=psum[:, b, 0:128].rearrange("p (s j) -> p s j", j=8), func=AF.Copy),
                      reads=[Rps[b]], writes=[Ru[ui]])
            if after_tb is not None:
                after_tb(tb)
        b = nb()
        for hh in range(2):
            em.op("pe", "transpose", dict(out=psum[:, b, hh * 120:(hh + 1) * 120],
                                          in_=stp[hh][0:120, g * 128:(g + 1) * 128], identity=identf[0:120, 0:120]),
                  reads=[Rconst, Ridentf], writes=[Rps[b]], sig=(hh == 1))
        em.op("act", "activation", dict(out=us_view(U)[:, :, 0:15],
                                        in_=psum[:, b, 0:240].rearrange("p (s r) -> p s r", r=15), func=AF.Copy),
              reads=[Rps[b]], writes=[Ru[ui]])
        b = nb()
        for j, t in enumerate([15, 16]):
            for k in range(8):
                em.op("pe", "matmul", dict(out=psum[:, b, j * 128:(j + 1) * 128], lhsT=hnT[:, k, t * 128:(t + 1) * 128],
                                           rhs=wslot[slot][:, k, :], start=(k == 0), stop=(k == 7)),
                      reads=[Rw[slot], Rh[t]], writes=[Rps[b]], sig=(k == 7 and j == 1))
        em.op("act", "activation", dict(out=ustage[:, :, g * 128:(g + 1) * 128],
                                        in_=psum[:, b, 0:256].rearrange("p (j c) -> p j c", j=2), func=AF.Copy),
              reads=[Rps[b]], writes=[Rust])
        w_done(slot)

    Rhd_p = Res()

    def s2_back_pool(g):
        ui = g % 2
        U = ubuf[ui]
        srcs = [(U, Ru[ui]), (tbuf[0], Rt[0]), (tbuf[1], Rt[1]), (tbuf[0], Rt[0]), (tbuf[1], Rt[1])]
        sh = 1
        for lvl in range(g + 1):
            src, rsrc = srcs[lvl]
            dst, rdst = srcs[lvl + 1]
            lo = 2 * sh - 1
            em.op("pool", "tensor_tensor", dict(out=dst[:, lo:15 + SEQ], in0=src[:, lo:15 + SEQ],
                                                in1=src[:, lo - sh:15 + SEQ - sh], op=ALU.add),
                  reads=[rsrc], writes=[rdst])
            em.op("pool", "tensor_tensor", dict(out=us_view(dst)[:, :, lo:23], in0=us_view(src)[:, :, lo:23],
                                                in1=us_view(src)[:, :, lo - sh:23 - sh], op=ALU.add),
                  reads=[rsrc], writes=[rdst])
            sh *= 2

    def s2_back_d(g):
        ui = g % 2
        U = ubuf[ui]
        win, rwin = [(tbuf[0], Rt[0]), (tbuf[1], Rt[1]), (tbuf[0], Rt[0]), (tbuf[1], Rt[1])][g]
        di = g % 2
        dd = dbuf[di]
        rw = rcnt[:, g, 15:16]
        em.op("pool", "tensor_tensor", dict(out=win[:, 31:15 + SEQ], in0=win[:, 31:15 + SEQ],
                                            in1=rw.to_broadcast([128, SEQ - 16]), op=ALU.mult),
              reads=[Rrcnt], writes=[rwin])
        em.op("pool", "tensor_tensor", dict(out=win[:, 15:31], in0=win[:, 15:31], in1=rcnt[:, g, :], op=ALU.mult),
              reads=[Rrcnt], writes=[rwin])
        em.op("pool", "tensor_tensor", dict(out=us_view(win)[:, :, 15:23], in0=us_view(win)[:, :, 15:23],
                                            in1=rw.unsqueeze(2).to_broadcast([128, 16, 8]), op=ALU.mult),
              reads=[Rrcnt], writes=[rwin])
        em.op("pool", "tensor_tensor", dict(out=dd[:, 0:SEQ], in0=win[:, 15:15 + SEQ], in1=U[:, 15:15 + SEQ], op=ALU.subtract),
              reads=[rwin, Ru[ui]], writes=[Rd[di]])
        em.op("pool", "tensor_tensor", dict(out=dd[:, SEQ:NTOK].rearrange("p (s j) -> p s j", j=8),
                                            in0=us_view(win)[:, :, 15:23], in1=us_view(U)[:, :, 15:23], op=ALU.subtract),
              reads=[rwin, Ru[ui]], writes=[Rd[di]])

    def s2_gmm(g):
        di = g % 2
        dd = dbuf[di]
        for tb in range(5):
            n = NB[tb]
            b = nb()
            em.op("pe", "matmul", dict(out=psum[:, b, 0:n], lhsT=wg_sb[:, g, :], rhs=dd[:, CB[tb]:CB[tb] + n],
                                       start=True, stop=True),
                  reads=[Rwg, Rd[di]], writes=[Rps[b]])
            if tb % 2 == 0:
                em.op("act", "activation", dict(out=pool_out[:, g, CB[tb]:CB[tb] + n], in_=psum[:, b, 0:n], func=AF.Copy,
                                                scale=pscale[:, g:g + 1]),
                      reads=[Rps[b], Rg0], writes=[Rpo[g][tb]])
            else:
                em.op("dve", "tensor_scalar", dict(out=pool_out[:, g, CB[tb]:CB[tb] + n], in0=psum[:, b, 0:n],
                                                   scalar1=pscale[:, g:g + 1], scalar2=None, op0=ALU.mult),
                      reads=[Rps[b], Rg0], writes=[Rpo[g][tb]])


    csc = [0]

    def s3_chunk(cc):
        for _ in s3_chunk_gen(cc):
            pass

    def s3_chunk_gen(cc):
        vi = cc % 2
        V = vbuf[vi]
        if cc > 0:
            w_issue()
        sl_b = load_w(); sl_c = load_w(); sl_h = load_w()
        b = nb()
        em.op("pe", "transpose", dict(out=psum[:, b, 0:32], in_=stc[0:32, cc * 128:(cc + 1) * 128], identity=identf[0:32, 0:32]),
              reads=[Rconst, Ridentf], writes=[Rps[b]])
        em.op("act", "activation", dict(out=vs_view(V)[:, :, 0:2],
                                        in_=psum[:, b, 0:32].rearrange("p (s r) -> p s r", r=2), func=AF.Copy),
              reads=[Rps[b]], writes=[Rvpre[vi]])
        for tb in range(5):
            yield tb
            n = NB[tb]
            hres = [Rh[t] for t in tiles_of(tb)]
            rhs = lambda k, tb=tb, n=n: hnT[:, k, CB[tb]:CB[tb] + n]
            bc = nb(); mm_fm(bc, n, sl_c, 8, rhs, hres)
            bh = nb(); mm_fm(bh, n, sl_h, 8, rhs, hres)
            bb = nb(); mm_fm(bb, n, sl_b, 8, rhs, hres)
            ci = csc[0] % 2
            y0i = (csc[0] % 2) * 2
            csc[0] += 1
            cs_ = csb[ci]
            em.op("act", "activation", dict(out=cs_[:, 0:n], in_=psum[:, bc, 0:n], func=AF.Copy),
                  reads=[Rps[bc]], writes=[Rcsb[ci]])
            if tb < 4:
                c0 = CB[tb]
                em.op("dve", "tensor_tensor", dict(out=V[:, 2 + c0:2 + c0 + 512], in0=psum[:, bh, :], in1=cs_[:, :], op=ALU.mult),
                      reads=[Rps[bh], Rcsb[ci]], writes=[Rv[vi][tb]])
                vrd = [Rv[vi][tb], Rvpre[vi]] + ([Rv[vi][tb - 1]] if tb > 0 else [])
                v0 = V[:, c0:c0 + 512]; v1 = V[:, c0 + 1:c0 + 513]; v2 = V[:, c0 + 2:c0 + 514]
                ya = ybuf[y0i][:, :]; yb = ybuf[y0i + 1][:, :]
                bsrc = psum[:, bb, :]
                cdst = conv_out[:, cc, c0:c0 + 512]
            else:
                em.op("dve", "tensor_tensor", dict(out=vs_view(V)[:, :, 2:10],
                                                   in0=psum[:, bh, 0:128].rearrange("p (s j) -> p s j", j=8),
                                                   in1=cs_[:, 0:128].rearrange("p (s j) -> p s j", j=8), op=ALU.mult),
                      reads=[Rps[bh], Rcsb[ci], Rvpre[vi]], writes=[Rv[vi][tb]])
                vrd = [Rv[vi][tb], Rvpre[vi]]
                v0 = vs_view(V)[:, :, 0:8]; v1 = vs_view(V)[:, :, 1:9]; v2 = vs_view(V)[:, :, 2:10]
                ya = ybuf[y0i][:, 0:128].rearrange("p (s j) -> p s j", j=8)
                yb = ybuf[y0i + 1][:, 0:128].rearrange("p (s j) -> p s j", j=8)
                bsrc = psum[:, bb, 0:128].rearrange("p (s j) -> p s j", j=8)
                cdst = conv_out[:, cc, SEQ:NTOK].rearrange("p (s j) -> p s j", j=8)
            Rya = Ry[y0i]; Ryb = Ry[y0i + 1]
            em.op("act", "activation", dict(out=ya, in_=v0, func=AF.Copy, scale=wconv[:, cc, 0:1]),
                  reads=vrd + [Rwc], writes=[Rya])
            em.op("dve", "scalar_tensor_tensor", dict(out=yb, in0=v1, scalar=wconv[:, cc, 1:2], in1=ya, op0=ALU.mult, op1=ALU.add),
                  reads=vrd + [Rya, Rwc], writes=[Ryb])
            em.op("dve", "scalar_tensor_tensor", dict(out=ya, in0=v2, scalar=wconv[:, cc, 2:3], in1=yb, op0=ALU.mult, op1=ALU.add),
                  reads=vrd + [Ryb, Rwc], writes=[Rya])
            em.op("dve", "tensor_tensor", dict(out=cdst, in0=bsrc, in1=ya, op=ALU.mult),
                  reads=[Rps[bb], Rya], writes=[Rco[cc][tb]])
        b = nb()
        for j, t in enumerate([15, 16]):
            for wi_, sl in enumerate([sl_c, sl_h]):
                for k in range(8):
                    em.op("pe", "matmul", dict(out=psum[:, b, (2 * j + wi_) * 128:(2 * j + wi_ + 1) * 128],
                                               lhsT=hnT[:, k, t * 128:(t + 1) * 128], rhs=wslot[sl][:, k, :],
                                               start=(k == 0), stop=(k == 7)),
                          reads=[Rw[sl], Rh[t]], writes=[Rps[b]], sig=(k == 7 and j == 1 and wi_ == 1))
        pv = psum[:, b, :].rearrange("p (j w c) -> p j w c", j=2, w=2)
        em.op("act", "activation", dict(out=ctm[:, :, :], in_=pv[:, :, 0, :], func=AF.Copy), reads=[Rps[b]], writes=[Rctm])
        em.op("dve", "tensor_tensor", dict(out=vstage[:, :, cc * 128:(cc + 1) * 128], in0=pv[:, :, 1, :], in1=ctm[:, :, :], op=ALU.mult),
              reads=[Rps[b], Rctm], writes=[Rvst])
        w_done(sl_b, sl_c, sl_h)

    g3 = s3_chunk_gen(0)

    def f0_before(tb):
        s1_upto(tiles_of(tb)[-1])
        if tb == 0:
            next(g3)
        if tb == 2:
            w_issue()

    def f0_after(tb):
        try:
            next(g3)
        except StopIteration:
            pass

    def s1_hook():
        if s1k[0] < NT + 2:
            s1_step()

    s1_upto(5)
    mm_hook[0] = s1_hook
    s2_front(0, before_tb=f0_before, after_tb=f0_after)
    mm_hook[0] = None
    for g_ in range(4):
        for tb_ in range(5):
            Rpo[g_][tb_].inherit(*Rxin1[4:])
    s2_front(1)
    s2_back_pool(0); s2_back_d(0)
    s2_back_pool(1); s2_back_d(1); w_issue()
    s3_chunk(1)
    s2_front(2)
    s2_front(3)
    em.dma("sp", npp[:, :], ustage[113:128, 0, :], Dust, reads=[Rust], final=True)
    for s in range(16):
        em.dma("sp", nps[s, 7:15, :], ustage[s * 8:(s + 1) * 8, 1, :], Dust, reads=[Rust], final=True)
    s2_gmm(0)
    s2_back_pool(2); s2_back_d(2)
    s2_gmm(1)
    s2_back_pool(3); s2_back_d(3); w_issue()
    s3_chunk(2)
    s3_chunk(3)
    s2_gmm(2)
    s2_gmm(3)
    em.dma("sp", nps[:, 0:7, :], spool[:, 8:15, :], Dout, final=True)
    em.dma("sp", ncp[:, :], vstage[126:128, 0, :], Dvst, reads=[Rvst], final=True)
    for s in range(16):
        em.dma("sp", ncs[s, :, :], vstage[s * 8 + 6:s * 8 + 8, 1, :], Dvst, reads=[Rvst], final=True)

    Rs4 = [Res() for _ in range(8)]
    for r_ in Rs4:
        r_.inherit(Rt[0], Rt[1])
    for m_ in range(8):
        for tb in range(5):
            Rm[m_][tb].inherit(*Rxin1, *Rxn1[0], *Rxn1[1], Rconst, Rust, Rvst)
    em.dma("pool", wo_sb[:, :, :], w_o.rearrange("(k p) n -> p k n", p=128), Dwo, writes=[Rwo])
    s4c = [0]

    def s4_chunk(m):
        w_issue()
        sl_gp = load_w(); sl_gc = load_w(); sl_pu = load_w(); sl_co = load_w()
        for tb in range(5):
            n = NB[tb]
            c0 = CB[tb]
            hres = [Rh[t] for t in tiles_of(tb)]
            b1 = nb(); mm_fm(b1, n, sl_gp, 8, lambda k, c0=c0, n=n: hnT[:, k, c0:c0 + n], hres)
            b2 = nb(); mm_fm(b2, n, sl_pu, 4, lambda k, c0=c0, n=n: pool_out[:, k, c0:c0 + n], [Rpo[k][tb] for k in range(4)])
            b3 = nb(); mm_fm(b3, n, sl_gc, 8, lambda k, c0=c0, n=n: hnT[:, k, c0:c0 + n], hres)
            b4 = nb(); mm_fm(b4, n, sl_co, 4, lambda k, c0=c0, n=n: conv_out[:, k, c0:c0 + n], [Rco[k][tb] for k in range(4)])
            q = (s4c[0] % 2) * 4
            s4c[0] += 1
            sgp, sgc, t1, t2 = s4buf[q], s4buf[q + 1], s4buf[q + 2], s4buf[q + 3]
            em.op("act", "activation", dict(out=sgp[:, 0:n], in_=psum[:, b1, 0:n], func=AF.Sigmoid),
                  reads=[Rps[b1]], writes=[Rs4[q]])
            em.op("dve", "tensor_tensor", dict(out=t1[:, 0:n], in0=psum[:, b2, 0:n], in1=sgp[:, 0:n], op=ALU.mult),
                  reads=[Rps[b2], Rs4[q]], writes=[Rs4[q + 2]])
            em.op("act", "activation", dict(out=sgc[:, 0:n], in_=psum[:, b3, 0:n], func=AF.Sigmoid),
                  reads=[Rps[b3]], writes=[Rs4[q + 1]])
            em.op("dve", "tensor_tensor", dict(out=t2[:, 0:n], in0=psum[:, b4, 0:n], in1=sgc[:, 0:n], op=ALU.mult),
                  reads=[Rps[b4], Rs4[q + 1]], writes=[Rs4[q + 3]])
            em.op("pool", "tensor_tensor", dict(out=merged[:, m, c0:c0 + n], in0=t1[:, 0:n], in1=t2[:, 0:n], op=ALU.add),
                  reads=[Rs4[q + 2], Rs4[q + 3]], writes=[Rm[m][tb]])
        w_done(sl_gp, sl_gc, sl_pu, sl_co)

    for m in range(8):
        s4_chunk(m)

    Rxin5 = [Res(), Res()]; Dxin5 = [em.newsem("dxin5a"), em.newsem("dxin5b")]
    Rxn5 = [Res(), Res(), Res()]
    allco = [Rco[k][tb] for k in range(4) for tb in range(5)]
    for r_ in Rxin5 + Rxn5:
        r_.inherit(*allco)
    alls4 = Rs4 + [Rd[0], Rd[1], Ru[0], Ru[1]] + [Rv[0][tb] for tb in range(5)] + [Rvpre[0]] + Rcsb + Ry[0:2] + [Rctm]
    for t in range(NT):
        Rx[t].inherit(*alls4)
    allpo = [Rpo[k][tb] for k in range(4) for tb in range(5)]

    def s5_a(t):
        i = t % 2
        tb = tb_of(t)
        em.dma("sp", xin5[i][:, :], xtile(t), Dxin5[i], writes=[Rxin5[i]])
        b0 = nb2()
        mm_tm(b0, lambda k, t=t: merged[:, k, t * 128:(t + 1) * 128], [Rm[k][tb] for k in range(8)],
              lambda k, half: wo_sb[:, k, half * 512:(half + 1) * 512], [Rwo], 8)
        em.op("dve", "tensor_tensor", dict(out=x_res[:, t, :], in0=psum[:, b0:b0 + 2, :].rearrange("p a b -> p (a b)"),
                                           in1=xin5[i][:, :], op=ALU.add),
              reads=[Rps[b0], Rps[b0 + 1], Rxin5[i]], writes=[Rx[t]])

    for k in range(NT + 3):
        if k < NT:
            s5_a(k)
        if 0 <= k - 1 < NT:
            norm_front(x_res[:, k - 1, :], Rx[k - 1], xn5[(k - 1) % 3], Rxn5[(k - 1) % 3])
        if 0 <= k - 3 < NT:
            norm_back(xn5[(k - 3) % 3], Rxn5[(k - 3) % 3], 1, k - 3)

    Ract = [[Res() for _ in range(5)] for _ in range(4)]
    for fl in range(4):
        for tb in range(5):
            Ract[fl][tb].inherit(*allpo)
    Rsg6 = [Res(), Res()]; Rxn6 = [Res(), Res(), Res()]
    for r_ in Rsg6 + Rxn6:
        r_.inherit(*allco, *Rxin5, *Rxn5)
    Rwfo = [Res(), Res()]; Dwfo = [em.newsem("dwfoa"), em.newsem("dwfob")]
    allm = [Rm[k][tb] for k in range(8) for tb in range(5)]
    for r_ in Rwfo:
        r_.inherit(*allm)
    Rwpg = Res().inherit(*allm); Dwpg = em.newsem("dwpg")
    Rwple = Res().inherit(Rwo); Dwple = em.newsem("dwple")
    Rgfin = Res().inherit(Rwo); Dgfin = em.newsem("dgfin")
    sgc6 = [0]

    def s6_in(part, f0, nf, fl):
        wi = part % 2
        f = f0 + fl
        w_issue()
        sl_g = load_w(); sl_u = load_w()
        if fl == 0:
            em.dma("pool", wfo[wi][:, 0:nf, :],
                   w_ffn_out[f0 * 128:(f0 + nf) * 128, :].rearrange("(f p) n -> p f n", p=128),
                   Dwfo[wi], writes=[Rwfo[wi]])
            if part == 5:
                for kh in range(2):
                    em.op("pool", "tensor_tensor", dict(
                        out=wpg_sb[:, 4 * kh:4 * kh + 4, :], in0=wpg_sb[:, 4 * kh:4 * kh + 4, :],
                        in1=gT[2][:, 4 * kh:4 * kh + 4].unsqueeze(2).to_broadcast([128, 4, D]), op=ALU.mult),
                        reads=[Rg0], writes=[Rwpg])
            if part == 4:
                em.dma("pool", wpg_sb[:, :, :], w_ple_gate.rearrange("(k p) n -> p k n", p=128), Dwpg, writes=[Rwpg])
                em.dma("pool", wple_sb[:, :, :], w_ple.rearrange("(k p) n -> p k n", p=128), Dwple, writes=[Rwple])
                em.dma("sp", gfin[:, :], g_final.partition_broadcast(128), Dgfin, writes=[Rgfin])
        for tb in range(5):
            n = NB[tb]
            c0 = CB[tb]
            hres = [Rh[t] for t in tiles_of(tb)]
            bg = nb(); mm_fm(bg, n, sl_g, 8, lambda k, c0=c0, n=n: hnT[:, k, c0:c0 + n], hres)
            bu = nb(); mm_fm(bu, n, sl_u, 8, lambda k, c0=c0, n=n: hnT[:, k, c0:c0 + n], hres)
            si = sgc6[0] % 2
            sgc6[0] += 1
            em.op("act", "activation", dict(out=sg6[si][:, 0:n], in_=psum[:, bg, 0:n], func=AF.Silu),
                  reads=[Rps[bg]], writes=[Rsg6[si]])
            em.op("dve", "tensor_tensor", dict(out=act_sb[:, fl, c0:c0 + n], in0=psum[:, bu, 0:n], in1=sg6[si][:, 0:n], op=ALU.mult),
                  reads=[Rps[bu], Rsg6[si]], writes=[Ract[fl][tb]])
        w_done(sl_g, sl_u)

    def s6_out(part, nf, t):
        wi = part % 2
        tb = tb_of(t)
        b0 = nb2()
        mm_tm(b0, lambda k, t=t: act_sb[:, k, t * 128:(t + 1) * 128], [Ract[k][tb] for k in range(nf)],
              lambda k, half, wi=wi: wfo[wi][:, k, half * 512:(half + 1) * 512], [Rwfo[wi]], nf)
        em.op("dve", "tensor_tensor", dict(out=x_res[:, t, :], in0=psum[:, b0:b0 + 2, :].rearrange("p a b -> p (a b)"),
                                           in1=x_res[:, t, :], op=ALU.add),
              reads=[Rps[b0], Rps[b0 + 1]], writes=[Rx[t]])

    f0 = 0
    for part, nf in enumerate(NPART):
        for fl in range(nf):
            s6_in(part, f0, nf, fl)
        if part < len(NPART) - 1:
            for t in range(NT):
                s6_out(part, nf, t)
        else:
            def s6_norm_steps(k):
                if 0 <= k - 1 < NT:
                    norm_front(x_res[:, k - 1, :], Rx[k - 1], xn6[(k - 1) % 3], Rxn6[(k - 1) % 3], scale_eng="pool")
                if 0 <= k - 3 < NT:
                    norm_back(xn6[(k - 3) % 3], Rxn6[(k - 3) % 3], 2, k - 3)

            for k in range(NT):
                s6_out(part, nf, k)
                s6_norm_steps(k)
        f0 += nf

    allact = [Ract[fl][tb] for fl in range(4) for tb in range(5)]
    mk7 = lambda n=2: [Res().inherit(*allact, *Rsg6, *Rxn6) for _ in range(n)]
    mkwo = lambda n: [Res().inherit(Rwo, Rwple, Rgfin) for _ in range(n)]
    Rsg7 = mk7(); Rtt7 = mk7(); Rpin = mkwo(4); Rpbf = mkwo(3); RpT = mkwo(2)
    Dpin = [em.newsem("dpin%d" % i) for i in range(4)]
    Dyo = [em.newsem("dyo%d" % i) for i in range(4)]
    s7b = {}
    c7 = statc[0]
    statc[0] += 2 * NT
    GRP = 2

    def s7_pin(t):
        if t < NT:
            em.dma("sp", pin[t % 4][:, :], ptile(t), Dpin[t % 4], writes=[Rpin[t % 4]])

    def s7_cast(t):
        if t < NT:
            em.op("pool", "tensor_tensor", dict(out=pbf[t % 3][:, :], in0=pin[t % 4][:, :], in1=yst[0][:, 0:256], op=ALU.add),
                  reads=[Rpin[t % 4], Rz7], writes=[Rpbf[t % 3]])

    def s7_a(t):
        i = t % 2
        s7_pin(t + 3)
        s7_cast(t + 1)
        b = nb()
        for c in range(2):
            em.op("pe", "transpose", dict(out=psum_bf[:, b, c * 128:(c + 1) * 128],
                                          in_=pbf[t % 3][:, c * 128:(c + 1) * 128], identity=ident[:, :]),
                  reads=[Rpbf[t % 3], Rident], writes=[Rps[b]], sig=(c == 1))
        em.op("dve", "tensor_copy", dict(out=pT[i][:, :, :], in_=psum_bf[:, b, 0:256].rearrange("p (c t) -> p c t", c=2)),
              reads=[Rps[b]], writes=[RpT[i]])

    def s7_mm(t):
        i = t % 2
        bg = nb2()
        mm_tm(bg, lambda k, t=t: hnT[:, k, t * 128:(t + 1) * 128], [Rh[t]],
              lambda k, half: wpg_sb[:, k, half * 512:(half + 1) * 512], [Rwpg], 8)
        bp = nb2()
        mm_tm(bp, lambda k, i=i: pT[i][:, k, :], [RpT[i]],
              lambda k, half: wple_sb[:, k, half * 512:(half + 1) * 512], [Rwple], 2)
        s7b[t] = (bg, bp)

    def s7_b(t):
        i = t % 2
        bg, bp = s7b[t]
        em.op("act", "activation", dict(out=sg7[i][:, :], in_=psum[:, bg:bg + 2, :].rearrange("p a b -> p (a b)"), func=AF.Sigmoid),
              reads=[Rps[bg], Rps[bg + 1]], writes=[Rsg7[i]])
        em.op("dve", "tensor_tensor", dict(out=tt7[i][:, :], in0=psum[:, bp:bp + 2, :].rearrange("p a b -> p (a b)"),
                                           in1=sg7[i][:, :], op=ALU.mult),
              reads=[Rps[bp], Rps[bp + 1], Rsg7[i]], writes=[Rtt7[i]])
        em.op("dve" if t == NT - 1 else "pool", "tensor_tensor",
              dict(out=x_res[:, t, :], in0=x_res[:, t, :], in1=tt7[i][:, :], op=ALU.add),
              reads=[Rtt7[i]], writes=[Rx[t]])

    Rms7 = {}

    def s7_c(t):
        i = t % 2
        g0 = (t // GRP) * GRP
        g1 = min(g0 + GRP, NT)
        if t == g0:
            Rms7[g0] = Res()
        Rms = Rms7[g0]
        em.op("act", "activation", dict(out=tt7[i][:, :], in_=x_res[:, t, :], func=AF.Square, scale=1.0 / 32.0,
                                        accum_out=stat[:, c7 + t:c7 + t + 1]),
              reads=[Rx[t]], writes=[Rtt7[i], Rms])
        if t == g1 - 1:
            ms = stat[:, c7 + g0:c7 + g1]
            rs = stat[:, c7 + NT + g0:c7 + NT + g1]
            Rrs = Res()
            em.op("act", "activation", dict(out=ms, in_=ms, func=AF.Sqrt, bias=EPS, scale=1.0), reads=[Rms], writes=[Rms])
            em.op("dve", "reciprocal", dict(out=rs, in_=ms), reads=[Rms], writes=[Rrs])
            for tt in range(g0, g1):
                if tt == NT - 1:
                    for hh in range(2):
                        cs = slice(hh * 512, (hh + 1) * 512)
                        Rxh = Res().inherit(Rx[tt])
                        em.op("dve", "scalar_tensor_tensor", dict(out=x_res[:, tt, cs], in0=x_res[:, tt, cs],
                                                                  scalar=stat[:, c7 + NT + tt:c7 + NT + tt + 1], in1=gfin[:, cs],
                                                                  op0=ALU.mult, op1=ALU.mult),
                              reads=[Rrs, Rgfin], writes=[Rxh])
                        em.dma("sp", ytile(tt)[:, cs], x_res[:, tt, cs], Dyo[(tt + 2 * hh) % 4], reads=[Rxh], final=True)
                    continue
                em.op("dve", "scalar_tensor_tensor", dict(out=x_res[:, tt, :], in0=x_res[:, tt, :],
                                                          scalar=stat[:, c7 + NT + tt:c7 + NT + tt + 1], in1=gfin[:, :],
                                                          op0=ALU.mult, op1=ALU.mult),
                      reads=[Rrs, Rgfin], writes=[Rx[tt]])
                em.dma("sp", ytile(tt), x_res[:, tt, :], Dyo[tt % 4], reads=[Rx[tt]], final=True)

    Rz7 = mk7(1)[0]
    em.op("pool", "memset", dict(ap=yst[0][:, 0:256], constant=0.0), writes=[Rz7])
    s7_pin(0)
    s7_pin(1)
    s7_pin(2)
    s7_cast(0)
    for k in range(NT + 2):
        if k < 3:
            s6_norm_steps(NT + k)
        if k < NT:
            s7_a(k)
        if 0 <= k - 1 < NT:
            s7_b(k - 1)
        if k < NT:
            s7_mm(k)
        if 0 <= k - 2 < NT:
            s7_c(k - 2)

    assert wst["taken"] == len(WL) and wst["issued"] == len(WL), wst
    em.finish()
    em.run()
    return nc


_NC_CACHE = {}


def kernel(x_prompt, x_sample, state_pool, state_conv, p_prompt, p_sample,
           g_mix, w_in, w_pool_group, pool_scale, w_pool_up, w_conv, w_conv_out, w_o,
           g_ffn, w_ffn_in, w_ffn_out, g_ple, w_ple, w_ple_gate, g_final):
    f = lambda a: np.ascontiguousarray(np.asarray(a, dtype=np.float32))
    x_prompt, x_sample, state_pool, state_conv, p_prompt, p_sample = map(
        f, (x_prompt, x_sample, state_pool, state_conv, p_prompt, p_sample))
    shared = dict(
        g_mix=f(g_mix)[0], w_in=f(w_in)[0], w_pg=f(w_pool_group)[0], pool_scale=f(pool_scale)[0],
        w_pool_up=f(w_pool_up)[0], w_conv=f(w_conv)[0], w_conv_out=f(w_conv_out)[0], w_o=f(w_o)[0],
        g_ffn=f(g_ffn)[0], w_ffn_in=f(w_ffn_in)[0], w_ffn_out=f(w_ffn_out)[0], g_ple=f(g_ple)[0],
        w_ple=f(w_ple)[0], w_ple_gate=f(w_ple_gate)[0], g_final=f(g_final))
    nc = build_nc()
    in_maps = []
    for c in range(8):
        m = dict(shared)
        m["xp"] = x_prompt[c]
        m["xs"] = x_sample[16 * c:16 * (c + 1)].reshape(128, D)
        m["spool"] = state_pool[0, 16 * c:16 * (c + 1)]
        m["sconv"] = state_conv[0, 16 * c:16 * (c + 1)]
        m["ppr"] = p_prompt[0, c]
        m["psa"] = p_sample[0, 16 * c:16 * (c + 1)].reshape(128, 256)
        in_maps.append(m)
    res = run_bass_kernel_spmd(nc, in_maps, core_ids=list(range(8)))
    R = res.results
    y_prompt = np.stack([np.asarray(R[c]["yp"], dtype=np.float32) for c in range(8)], axis=0)
    y_sample = np.concatenate([np.asarray(R[c]["ys"], dtype=np.float32).reshape(16, 8, D) for c in range(8)], axis=0)
    npp = np.stack([np.asarray(R[c]["npp"], dtype=np.float32) for c in range(8)], axis=0)[None]
    ncp = np.stack([np.asarray(R[c]["ncp"], dtype=np.float32) for c in range(8)], axis=0)[None]
    nps = np.concatenate([np.asarray(R[c]["nps"], dtype=np.float32) for c in range(8)], axis=0)[None]
    ncs = np.concatenate([np.asarray(R[c]["ncs"], dtype=np.float32) for c in range(8)], axis=0)[None]
    return (y_prompt, y_sample, npp, ncp, nps, ncs)
```
